# Optimizing a Trainium2 kernel written in Bass

```python
import math
import jax, jax.numpy as jnp
from jax import lax
import numpy as np


D_MODEL = 2048
BATCH = 8
SEQ = 2048
DEPTH = 2

GRID_W = 64
CTX_LEN = 256
MIX_WIDTH = D_MODEL
ATT_WIDTH = MIX_WIDTH // 2
ATT_V_DIM = 128
ATT_QK_DIM = 64
ATT_HEADS = ATT_WIDTH // ATT_V_DIM
MLSTM_WIDTH = MIX_WIDTH - ATT_WIDTH
MLSTM_V_DIM = 256
MLSTM_QK_DIM = 128
MLSTM_HEADS = MLSTM_WIDTH // MLSTM_V_DIM
MLSTM_CHUNK = 64
CONV_WIDTH = 3
GATE_SOFTCAP = 15.0
ROPE_BASE = 10000.0
Q_BLOCK = 128
EPS = 1e-6

IN_SPLITS = (
    ATT_HEADS * 2 * ATT_QK_DIM,
    ATT_HEADS * 2 * ATT_QK_DIM,
    ATT_WIDTH,
    ATT_WIDTH,
    MLSTM_HEADS * MLSTM_QK_DIM,
    MLSTM_HEADS * MLSTM_QK_DIM,
    MLSTM_WIDTH,
    MLSTM_WIDTH,
    MLSTM_WIDTH,
    4 * MLSTM_HEADS,
)
IN_COLS = sum(IN_SPLITS)

kernel_name = 'hybrid_diffattn_mlstm_block'


def rms_norm(x, w):
    xf = x.astype(jnp.float32)
    y = xf * lax.rsqrt(jnp.mean(xf * xf, axis=-1, keepdims=True) + EPS)
    return (y * w.astype(jnp.float32)).astype(x.dtype)


def split_cols(p):
    idx = np.cumsum(IN_SPLITS)[:-1].tolist()
    return jnp.split(p, idx, axis=-1)


def axial_rope_tables(n_tokens):
    rows = n_tokens // GRID_W
    row = jnp.repeat(jnp.arange(rows), GRID_W).astype(jnp.float32)
    col = jnp.tile(jnp.arange(GRID_W), rows).astype(jnp.float32)
    n_freq = ATT_QK_DIM // 4
    inv = ROPE_BASE ** (-jnp.arange(n_freq, dtype=jnp.float32) / n_freq)
    ang = jnp.concatenate([row[:, None] * inv, col[:, None] * inv], axis=-1)
    return jnp.cos(ang), jnp.sin(ang)


def apply_rope(x, cos, sin):
    xf = x.astype(jnp.float32)
    x1, x2 = xf[..., 0::2], xf[..., 1::2]
    c = cos[:, None, None, :]
    s = sin[:, None, None, :]
    out = jnp.stack([x1 * c - x2 * s, x1 * s + x2 * c], axis=-1).reshape(x.shape)
    return out.astype(x.dtype)


def diff_softmax_mix(q, k, v, lam):
    s = jnp.einsum('bqhcd,bkhcd->bhcqk', q, k).astype(jnp.float32) * (ATT_QK_DIM ** -0.5)
    p = jax.nn.softmax(s, axis=-1)
    a = p[:, :, 0] - lam * p[:, :, 1]
    return jnp.einsum('bhqk,bkhe->bqhe', a.astype(v.dtype), v)


def diff_attention_branch(lat, ctx, lam_q1, lam_k1, lam_q2, lam_k2, subln_w, layer_idx, need_ctx_out):
    q, k, v, g = lat
    qc, kc, vc, gc = ctx
    B, T = q.shape[:2]
    f32 = jnp.float32
    lam_init = 0.8 - 0.6 * math.exp(-0.3 * layer_idx)
    lam = (jnp.exp(jnp.sum(lam_q1.astype(f32) * lam_k1.astype(f32)))
           - jnp.exp(jnp.sum(lam_q2.astype(f32) * lam_k2.astype(f32))) + lam_init)
    qk_heads = lambda t: t.reshape(t.shape[0], t.shape[1], ATT_HEADS, 2, ATT_QK_DIM)
    v_heads = lambda t: t.reshape(t.shape[0], t.shape[1], ATT_HEADS, ATT_V_DIM)
    cos, sin = axial_rope_tables(T)
    q = apply_rope(qk_heads(q), cos, sin)
    k = apply_rope(qk_heads(k), cos, sin)
    kc, vc = qk_heads(kc), v_heads(vc)
    k_all = jnp.concatenate([kc, k], axis=1)
    v_all = jnp.concatenate([vc, v_heads(v)], axis=1)
    n_blocks = T // Q_BLOCK
    q_blocks = jnp.swapaxes(q.reshape(B, n_blocks, Q_BLOCK, ATT_HEADS, 2, ATT_QK_DIM), 0, 1)
    o = lax.map(lambda qb: diff_softmax_mix(qb, k_all, v_all, lam), q_blocks)
    o = jnp.swapaxes(o, 0, 1).reshape(B, T, ATT_HEADS, ATT_V_DIM)

    def finish(o, gate):
        o = rms_norm(o, subln_w) * (1.0 - lam_init)
        return o.reshape(o.shape[0], o.shape[1], ATT_WIDTH) * jax.nn.silu(gate)

    y = finish(o, g)
    yc = finish(diff_softmax_mix(qk_heads(qc), kc, vc, lam), gc) if need_ctx_out else None
    return y, yc


def short_conv(x, w, b):
    pad = CONV_WIDTH // 2
    T = x.shape[1]
    xp = jnp.pad(x, ((0, 0), (pad, pad), (0, 0)))
    y = b
    for j in range(CONV_WIDTH):
        y = y + w[j] * xp[:, j:j + T]
    return y


def mlstm_chunkwise(q, k, v, ig, lf, state):
    B, H, T, _ = q.shape
    dv = v.shape[-1]
    nc = T // MLSTM_CHUNK

    def chunks(t):
        return jnp.moveaxis(t.reshape(B, H, nc, MLSTM_CHUNK, *t.shape[3:]), 2, 0)

    tril = jnp.tril(jnp.ones((MLSTM_CHUNK, MLSTM_CHUNK), dtype=bool))

    def step(carry, xs):
        C, n, m = carry
        qc, kc, vc, ic, fc = xs
        b = jnp.cumsum(fc, axis=-1)
        logw = jnp.where(tril, b[..., :, None] - b[..., None, :] + ic[..., None, :], -jnp.inf)
        inter = b + m[..., None]
        m_row = jnp.maximum(inter, jnp.max(logw, axis=-1))
        w = jnp.exp(logw - m_row[..., None])
        s = jnp.einsum('bhld,bhsd->bhls', qc, kc) * w
        sc = jnp.exp(inter - m_row)
        numer = sc[..., None] * jnp.einsum('bhld,bhde->bhle', qc, C) + jnp.einsum('bhls,bhse->bhle', s, vc)
        denom = sc * jnp.einsum('bhld,bhd->bhl', qc, n) + jnp.sum(s, axis=-1)
        h = numer / jnp.maximum(jnp.abs(denom), jnp.exp(-m_row))[..., None]
        b_last = b[..., -1]
        g = b_last[..., None] - b + ic
        m_new = jnp.maximum(b_last + m, jnp.max(g, axis=-1))
        wk = jnp.exp(g - m_new[..., None])
        decay = jnp.exp(b_last + m - m_new)
        C_new = decay[..., None, None] * C + jnp.einsum('bhsd,bhse->bhde', wk[..., None] * kc, vc)
        n_new = decay[..., None] * n + jnp.einsum('bhs,bhsd->bhd', wk, kc)
        return (C_new, n_new, m_new), h

    state, h = lax.scan(step, state, (chunks(q), chunks(k), chunks(v), chunks(ig), chunks(lf)))
    h = jnp.moveaxis(h, 0, 2).reshape(B, H, T, dv)
    return h, state


def mlstm_branch(lat, ctx, conv_w, conv_b, i_bias, f_bias, head_norm_w, need_ctx_out):
    f32 = jnp.float32

    def prep(q, k, v, gates):
        qk = jax.nn.silu(short_conv(jnp.concatenate([q, k], axis=-1), conv_w, conv_b))
        q, k = jnp.split(qk, 2, axis=-1)
        B, T = q.shape[:2]
        heads = lambda t, d: t.reshape(B, T, MLSTM_HEADS, d).transpose(0, 2, 1, 3).astype(f32)
        q = heads(q, MLSTM_QK_DIM)
        k = heads(k, MLSTM_QK_DIM) * (MLSTM_QK_DIM ** -0.5)
        v = heads(v, MLSTM_V_DIM)
        gp = gates.astype(f32).reshape(B, T, 2, 2, MLSTM_HEADS)
        cap = lambda t: GATE_SOFTCAP * jnp.tanh(t / GATE_SOFTCAP)
        ig = cap(gp[:, :, 0] + i_bias.astype(f32))
        lf = jax.nn.log_sigmoid(cap(gp[:, :, 1] + f_bias.astype(f32)))
        return q, k, v, ig.transpose(2, 0, 3, 1), lf.transpose(2, 0, 3, 1)

    ql, kl, vl, igl, lfl = prep(lat[0], lat[1], lat[2], lat[5])
    qc, kc, vc, igc, lfc = prep(ctx[0], ctx[1], ctx[2], ctx[5])
    B = ql.shape[0]
    h_lat, h_ctx = [], []
    for d in range(2):
        flip = (lambda t: jnp.flip(t, axis=2)) if d == 1 else (lambda t: t)
        state0 = (jnp.zeros((B, MLSTM_HEADS, MLSTM_QK_DIM, MLSTM_V_DIM), f32),
                  jnp.zeros((B, MLSTM_HEADS, MLSTM_QK_DIM), f32),
                  jnp.zeros((B, MLSTM_HEADS), f32))
        hc, state = mlstm_chunkwise(flip(qc), flip(kc), flip(vc), flip(igc[d]), flip(lfc[d]), state0)
        hl, _ = mlstm_chunkwise(flip(ql), flip(kl), flip(vl), flip(igl[d]), flip(lfl[d]), state)
        h_lat.append(flip(hl))
        h_ctx.append(flip(hc))

    def finish(h, o, g):
        Bh, H, T, dv = h.shape
        h = rms_norm(h.transpose(0, 2, 1, 3), head_norm_w.reshape(H, dv)).reshape(Bh, T, H * dv)
        return h.astype(o.dtype) * jax.nn.sigmoid(o) * jax.nn.silu(g)

    y = finish(h_lat[0] + h_lat[1], lat[3], lat[4])
    yc = finish(h_ctx[0] + h_ctx[1], ctx[3], ctx[4]) if need_ctx_out else None
    return y, yc


def hybrid_layer(x, xc, mod, mod_c, w_in, w_out, norm_pre, norm_post,
                 lam_q1, lam_k1, lam_q2, lam_k2, attn_subln,
                 conv_w, conv_b, i_bias, f_bias, mlstm_norm, layer_idx, need_ctx_out):
    shift, scale, gate = jnp.split(mod, 3, axis=-1)
    shift_c, scale_c, gate_c = jnp.split(mod_c, 3, axis=-1)
    h = rms_norm(x, norm_pre) * (1.0 + scale[:, None]) + shift[:, None]
    hc = rms_norm(xc, norm_pre) * (1.0 + scale_c) + shift_c
    pl = split_cols(h @ w_in)
    pc = split_cols(hc @ w_in)
    ya, yac = diff_attention_branch(pl[:4], pc[:4], lam_q1, lam_k1, lam_q2, lam_k2,
                                    attn_subln, layer_idx, need_ctx_out)
    ym, ymc = mlstm_branch(pl[4:], pc[4:], conv_w, conv_b, i_bias, f_bias, mlstm_norm, need_ctx_out)
    y = jnp.concatenate([ya, ym], axis=-1) @ w_out
    x = x + gate[:, None] * rms_norm(y, norm_post)
    if need_ctx_out:
        yc = jnp.concatenate([yac, ymc], axis=-1) @ w_out
        xc = xc + gate_c * rms_norm(yc, norm_post)
    return x, xc


def setup_inputs(seed: int = 0) -> dict:
    key = jax.random.key(seed)
    ks = jax.random.split(key, 20)
    f32 = jnp.float32
    nrm = lambda k, shape: jax.random.normal(k, shape, dtype=f32)
    return {
        'x': nrm(ks[0], (BATCH, SEQ, D_MODEL)),
        'c': nrm(ks[1], (BATCH, D_MODEL)),
        'ctx': nrm(ks[2], (BATCH, CTX_LEN, D_MODEL)),
        'c_ctx': nrm(ks[3], (D_MODEL,)),
        'w_ada': nrm(ks[4], (DEPTH, D_MODEL, 3 * D_MODEL)) * (0.5 * D_MODEL ** -0.5),
        'b_ada': nrm(ks[5], (DEPTH, 3 * D_MODEL)) * 0.02,
        'norm_pre': 1.0 + 0.05 * nrm(ks[6], (DEPTH, D_MODEL)),
        'norm_post': 1.0 + 0.05 * nrm(ks[7], (DEPTH, D_MODEL)),
        'w_in': nrm(ks[8], (DEPTH, D_MODEL, IN_COLS)) * (D_MODEL ** -0.5),
        'w_out': nrm(ks[9], (DEPTH, MIX_WIDTH, D_MODEL)) * (MIX_WIDTH ** -0.5),
        'lam_q1': 0.1 * nrm(ks[10], (DEPTH, ATT_QK_DIM)),
        'lam_k1': 0.1 * nrm(ks[11], (DEPTH, ATT_QK_DIM)),
        'lam_q2': 0.1 * nrm(ks[12], (DEPTH, ATT_QK_DIM)),
        'lam_k2': 0.1 * nrm(ks[13], (DEPTH, ATT_QK_DIM)),
        'attn_subln': 1.0 + 0.05 * nrm(ks[14], (DEPTH, ATT_V_DIM)),
        'conv_w': nrm(ks[15], (DEPTH, CONV_WIDTH, 2 * MLSTM_HEADS * MLSTM_QK_DIM)) * (CONV_WIDTH ** -0.5),
        'conv_b': 0.02 * nrm(ks[16], (DEPTH, 2 * MLSTM_HEADS * MLSTM_QK_DIM)),
        'i_bias': 0.1 * nrm(ks[17], (DEPTH, 2, MLSTM_HEADS)),
        'f_bias': jnp.linspace(3.0, 6.0, MLSTM_HEADS, dtype=f32) + 0.1 * nrm(ks[18], (DEPTH, 2, MLSTM_HEADS)),
        'mlstm_norm': 1.0 + 0.05 * nrm(ks[19], (DEPTH, MLSTM_WIDTH)),
    }


def reference(x, c, ctx, c_ctx, w_ada, b_ada, norm_pre, norm_post, w_in, w_out,
              lam_q1, lam_k1, lam_q2, lam_k2, attn_subln, conv_w, conv_b,
              i_bias, f_bias, mlstm_norm):
    xc = ctx
    sc = jax.nn.silu(c)
    sc_ctx = jax.nn.silu(c_ctx)
    for l in range(DEPTH):
        mod = sc @ w_ada[l] + b_ada[l]
        mod_c = sc_ctx @ w_ada[l] + b_ada[l]
        x, xc = hybrid_layer(x, xc, mod, mod_c, w_in[l], w_out[l], norm_pre[l], norm_post[l],
                             lam_q1[l], lam_k1[l], lam_q2[l], lam_k2[l], attn_subln[l],
                             conv_w[l], conv_b[l], i_bias[l], f_bias[l], mlstm_norm[l],
                             l, l < DEPTH - 1)
    return x
```

```python
import math
import numpy as np
import concourse.bass as bass
import concourse.mybir as mybir
from concourse.bass_utils import run_bass_kernel_spmd
from contextlib import ExitStack

F32 = mybir.dt.float32
BF16 = mybir.dt.bfloat16
AF = mybir.ActivationFunctionType
ALU = mybir.AluOpType
AX = mybir.AxisListType

D = 2048
T = 2048
CTX = 256
NT = T + CTX
NTILE = NT // 128
DEPTH = 2
EPS = 1e-6
NSLOT_IN = 32
NSLOT_ADA = 24
LN_KS = math.log(128.0 ** -0.5)
NEG = -30000.0

ENGS = ["pe", "act", "dve", "pool", "sp"]
EPOCH = 12000
NDMA = 24
STRICT_SAME_ENGINE = True


class Tk:
    __slots__ = ("w", "r", "rd", "ps")

    def __init__(self):
        self.w = None
        self.r = {}
        self.rd = []
        self.ps = False


class Buf:
    def __init__(self, t):
        self.t = t
        self.k = Tk()

    def __getitem__(self, i):
        return self.t[i]


class Op:
    __slots__ = ("eng", "fn", "deps", "signal", "sem", "val", "dma")


class Prog:
    def __init__(self, nc, es):
        self.nc = nc
        self.es = es
        self.q = {e: [] for e in ENGS}
        self.dsem = [es.enter_context(nc.semaphore(f"dq{i}")) for i in range(NDMA)]
        self.dcnt = [0] * NDMA
        self.dlast = [None] * NDMA
        self.drr_sp = 0
        self.drr_pool = 0
        self.nsem = 0

    def _rec(self, o, reads, writes):
        eng = o.eng
        reads = [getattr(t, "k", t) for t in reads]
        writes = [getattr(t, "k", t) for t in writes]
        deps = {}
        raw = set()
        for t in reads:
            if t.w is not None:
                deps[id(t.w)] = t.w
                raw.add(id(t.w))
            if t.ps:
                for re_, r in t.r.items():
                    if re_ != eng:
                        deps[id(r)] = r
        for t in writes:
            if t.w is not None:
                deps[id(t.w)] = t.w
            for r in t.r.values():
                deps[id(r)] = r
            for r in t.rd:
                deps[id(r)] = r
        for d in deps.values():
            if d is o:
                continue
            if (not o.dma) and (not d.dma) and d.eng == eng:
                if eng == "pe":
                    continue
                if id(d) not in raw and not STRICT_SAME_ENGINE:
                    continue
            o.deps.append(d)
            d.signal = True
        for t in reads:
            if o.dma:
                t.rd.append(o)
            else:
                t.r[eng] = o
        for t in writes:
            t.w = o
            t.r = {}
            t.rd = []
        self.q[eng].append(o)
        return o

    def op(self, eng, fn, reads=(), writes=()):
        o = Op()
        o.eng = eng
        o.fn = fn
        o.deps = []
        o.signal = False
        o.dma = False
        o.sem = None
        o.val = 0
        return self._rec(o, reads, writes)

    def dma(self, eng, out, in_, reads=(), writes=()):
        o = Op()
        o.eng = eng
        o.fn = lambda e: e.dma_start(out=out, in_=in_)
        o.deps = []
        o.signal = False
        o.dma = True
        half = NDMA // 2
        if eng == "sp":
            j = self.drr_sp
            self.drr_sp = (self.drr_sp + 1) % half
        else:
            j = half + self.drr_pool
            self.drr_pool = (self.drr_pool + 1) % (NDMA - half)
        self.dcnt[j] += 1
        o.sem = self.dsem[j]
        o.val = 16 * self.dcnt[j]
        if self.dlast[j] is not None:
            o.deps.append(self.dlast[j])
        self.dlast[j] = o
        return self._rec(o, reads, writes)

    def barrier(self):
        lasts = []
        for e in ENGS:
            for o in reversed(self.q[e]):
                if o.fn is not None and not o.dma:
                    lasts.append(o)
                    break
        lasts += [d for d in self.dlast if d is not None]
        for e in ENGS:
            o = Op()
            o.eng = e
            o.fn = None
            o.deps = list(lasts)
            o.signal = False
            o.dma = False
            o.sem = None
            o.val = 0
            for d in o.deps:
                d.signal = True
            self.q[e].append(o)

    def finish(self):
        o = Op()
        o.eng = "sp"
        o.fn = None
        o.deps = [d for d in self.dlast if d is not None]
        o.signal = False
        o.dma = False
        self.q["sp"].append(o)
        nc, es = self.nc, self.es
        for eng in ENGS:
            cnt = 0
            sems = []
            for o in self.q[eng]:
                if o.dma or o.fn is None or not o.signal:
                    continue
                ep = cnt // EPOCH
                if ep >= len(sems):
                    sems.append(es.enter_context(nc.semaphore(f"s_{eng}_{ep}")))
                o.sem = sems[ep]
                o.val = cnt % EPOCH + 1
                cnt += 1
        block = es.enter_context(nc.Block())

        def emit(eng, e):
            waited = {}
            for o in self.q[eng]:
                for d in o.deps:
                    k = id(d.sem)
                    if waited.get(k, 0) >= d.val:
                        continue
                    e.wait_ge(d.sem, d.val)
                    waited[k] = d.val
                if o.fn is None:
                    continue
                ins = o.fn(e)
                if o.dma:
                    ins.then_inc(o.sem, 16)
                elif o.signal:
                    ins.then_inc(o.sem, 1)

        @block.tensor
        def _(e):
            emit("pe", e)

        @block.scalar
        def _(e):
            emit("act", e)

        @block.vector
        def _(e):
            emit("dve", e)

        @block.gpsimd
        def _(e):
            emit("pool", e)

        @block.sync
        def _(e):
            emit("sp", e)


class _Stop(Exception):
    pass


def build_program(stage=None):
    nc = bass.Bass("TRN2", target_bir_lowering=False)
    ext = lambda n, s, d=F32: nc.dram_tensor(n, s, d, kind="ExternalInput").ap()
    x_d = ext("x", [T, D])
    ctx_d = ext("ctx", [CTX, D])
    c2_d = ext("c2fm", [128, 32])
    wada_d = ext("wada", [DEPTH, NSLOT_ADA, 128, 4096])
    win_d = ext("win", [DEPTH, NSLOT_IN, 128, 4096])
    wgate_d = ext("wgate", [DEPTH, 128, 256])
    wout_d = ext("wout", [DEPTH, 128, 16 * D])
    pfm_d = ext("pfm", [DEPTH, 128, 112])
    prow_d = ext("prow", [DEPTH, 400])
    mnorm_d = ext("mnorm", [DEPTH, 4, 256])
    cst_d = ext("cst", [128, 1024])
    rope_d = ext("rope", [2, 128, T])
    out_d = nc.dram_tensor("out", [T, D], F32, kind="ExternalOutput").ap()
    yscr_d = nc.dram_tensor("yscr", [NT, D], BF16).ap()
    x1_d = nc.dram_tensor("x1scr", [T, D], F32).ap()
    woutb_d = nc.dram_tensor("woutb", [DEPTH, 128, 16 * D], BF16).ap()
    woutb_k = [Tk() for _ in range(DEPTH)]
    yscr_k = Tk()
    x1_k = Tk()
    dbg_d = nc.dram_tensor("dbg", [128, 8192], F32, kind="ExternalOutput").ap() if stage is not None else None

    with ExitStack() as es:
        P = Prog(nc, es)
        sb = lambda n, s, d: Buf(es.enter_context(nc.sbuf_tensor("sb_" + n, s, d)))
        def ps(n, s, d):
            b = Buf(es.enter_context(nc.psum_tensor("ps_" + n, s, d)))
            b.k.ps = True
            return b

        hT = es.enter_context(nc.sbuf_tensor("sb_hT", [128, 16, NT], BF16))
        hT_k = [Tk() for _ in range(NTILE)]
        R = [sb(f"R{i}", [128, D], F32) for i in range(2)]
        W = [sb(f"W{i}", [128, 4096], BF16) for i in range(3)]
        G = [sb(f"G{i}", [128, 4640], BF16) for i in range(8)]
        GN = R[1]
        cf = sb("cf", [128, 512], F32)
        cb = sb("cb", [128, 1024], BF16)
        pfm = sb("pfm", [128, DEPTH, 112], F32)
        prow = sb("prow", [128, DEPTH, 400], F32)
        mnorm = sb("mnorm", [128, 256], F32)
        wg = sb("wg", [128, 256], BF16)
        sc2 = sb("sc2", [128, 32], BF16)
        c2f = sb("c2f", [128, 32], F32)
        modt = sb("modt", [128, DEPTH, 96], F32)
        Amod = sb("Amod", [128, DEPTH, 2, 16], F32)
        Shm = sb("Shm", [128, DEPTH, 2, 16], F32)
        Gtm = sb("Gtm", [128, DEPTH, 2, 16], F32)
        st = [sb(f"st{i}", [128, 16], F32) for i in range(4)]
        lamt = sb("lamt", [128, 8], F32)
        abias = sb("abias", [128, 4], F32)
        qkst = sb("qkst", [128, 16], F32)
        subw = sb("subw", [128, 128], F32)
        tmpA = [sb(f"tmpA{i}", [128, 512], F32) for i in range(2)]
        tmpB = [sb(f"tmpB{i}", [128, 512], F32) for i in range(2)]
        tbf = [sb(f"tbf{i}", [128, 512], BF16) for i in range(2)]

        carve_state = [5, 0]

        class View:
            def __init__(self, ap):
                self.ap = ap
                self.k = Tk()

            def __getitem__(self, i):
                return self.ap[i]

        def carve(shape, dt):
            n = 1
            for d_ in shape[1:]:
                n *= d_
            nb = n * (4 if dt == F32 else 2)
            nb = (nb + 7) // 8 * 8
            nel = nb // 2
            if carve_state[1] + nel > 4640:
                carve_state[0] += 1
                carve_state[1] = 0
            g, off = carve_state
            assert g <= 7, (shape, carve_state)
            carve_state[1] += nel
            ap = G[g][:, off:off + nel]
            if dt == F32:
                ap = ap.bitcast(F32)
            ap = ap[:, 0:n]
            if len(shape) == 3:
                ap = ap.rearrange("p (a b) -> p a b", b=shape[2])
            return View(ap)

        osb = [carve([128, 4, 128], F32) for i in range(1)]
        ybuf = [carve([128, 4, 256], BF16) for i in range(2)]
        gz = carve([128, NTILE, 16], F32)
        glf = carve([128, NTILE, 8], F32)
        gig = carve([128, NTILE, 8], F32)
        gcum = gz
        ga1 = carve([128, NTILE, 8], F32)
        gwk = carve([128, NTILE, 8], F32)
        gdec = carve([128, NTILE, 8], F32)
        geb = carve([128, NTILE, 8], F32)
        gw1 = carve([128, NTILE, 8], F32)
        SWT = [carve([128, 128], BF16) for i in range(4)]
        kwb = [carve([128, 128], BF16) for i in range(4)]
        numb = [carve([128, 257], F32) for i in range(4)]
        hsum = [carve([128, 256], F32) for i in range(2)]
        Cst = [carve([128, 257], F32) for i in range(2)]
        RING = 4
        Cring = [[carve([128, 257], BF16) for i in range(RING)] for dr_ in range(2)]
        class VW:
            def __init__(self, ap, k=None, psum=False):
                self.ap = ap
                self.k = k if k is not None else Tk()
                if psum:
                    self.k.ps = True

            def __getitem__(self, i):
                return self.ap[i]

        SCA = es.enter_context(nc.psum_tensor("ps_SCA", [128, 1024], F32))
        SCB = es.enter_context(nc.psum_tensor("ps_SCB", [128, 1024], F32))
        FBm = [ps(f"FB{i}", [128, 512], F32) for i in range(2, 6)]
        FB = [VW(SCA[:, 0:512], psum=True), VW(SCA[:, 512:1024], psum=True)] + FBm + \
             [VW(SCB[:, 0:512], psum=True), VW(SCB[:, 512:1024], psum=True)]
        HB = [VW(SCB[:, 0:512].bitcast(BF16), k=FB[6].k), VW(SCB[:, 512:1024].bitcast(BF16), k=FB[7].k)]
        SC2 = [SCA, SCB]
        SC2_k = [[FB[0].k, FB[1].k], [FB[6].k, FB[7].k]]

        ident_f = cf[:, 0:128]
        tri = [cf[:, 128:256], cf[:, 256:384]]
        ones_f = cf[:, 384:512]
        ident_b = cb[:, 0:128]
        nmk = [cb[:, 512:640], cb[:, 640:768]]
        bo_b = cb[:, 768:896]
        perm_b = cb[:, 896:1024]

        qT = G[0][:, 0:NT]
        kT = G[0][:, NT:2 * NT]
        qk_k = [[Tk() for _ in range(5)] for _ in range(2)]
        G0_all = [qk_k[p_][c_] for p_ in range(2) for c_ in range(5)]
        VaA = G[1][:, 0:NTILE * 129].rearrange("p (t c) -> p t c", c=129)
        sgA = G[1][:, 2322:2322 + NT].rearrange("p (t c) -> p t c", c=128)
        PTb = [G[2][:, i * 1024:(i + 1) * 1024] for i in range(4)]
        PT_k = [Tk() for _ in range(4)]
        xs = G[3][:, :].bitcast(F32)
        ys = G[4][:, :].bitcast(F32)
        VaM = G[1][:, 0:NTILE * 257].rearrange("p (t c) -> p t c", c=257)
        ogM = G[2][:, 0:NTILE * 256].rearrange("p (t c) -> p t c", c=256)
        hpart = G[3][:, 0:NTILE * 256].rearrange("p (t c) -> p t c", c=256)

        dma_in = lambda out, in_, w, eng="sp", r=(): P.dma(eng, out, in_, reads=r, writes=w)

        def ck(k, dumps=()):
            if stage == k:
                P.barrier()
                col = 0
                for (ap, n) in dumps:
                    P.dma("pool", dbg_d[0:ap.shape[0], col:col + n], ap)
                    col += n
                raise _Stop()

        def _body():

            dma_in(cf[:, :], cst_d[:, 0:512], [cf])
            dma_in(cb[:, :], cst_d, [cb], eng="pool")
            dma_in(c2f[:, :], c2_d, [c2f])
            for l in range(DEPTH):
                dma_in(pfm[:, l, :], pfm_d[l], [pfm])
                dma_in(prow[:, l, :], prow_d[l].partition_broadcast(128), [prow])
            P.op("act", lambda e: e.activation(out=sc2[:, :], in_=c2f[:, :], func=AF.Silu), [c2f], [sc2])

            wst = {"ctr": 0, "src": [], "issued": 0, "used": 0, "bufs": {}}

            def ws_begin(sources):
                assert wst["used"] == len(wst["src"]), (wst["used"], len(wst["src"]))
                wst["src"] = list(sources)
                wst["issued"] = 0
                wst["used"] = 0
                wst["bufs"] = {}

            def ws_next():
                while wst["issued"] < min(len(wst["src"]), wst["used"] + 3):
                    i = wst["ctr"] % 3
                    wst["ctr"] += 1
                    src = wst["src"][wst["issued"]]
                    for j in range(4):
                        P.dma("pool", W[i][:, j * 1024:(j + 1) * 1024], src[:, j * 1024:(j + 1) * 1024], writes=[W[i]])
                    wst["bufs"][wst["issued"]] = W[i]
                    wst["issued"] += 1
                b = wst["bufs"][wst["used"]]
                wst["used"] += 1
                return b

            def ada_slot(l, s_):
                wsl = ws_next()
                psm = FB[3]
                w3 = wsl[:, :].rearrange("p (c n) -> p c n", n=256)
                for g in range(2):
                    for c in range(16):
                        P.op("pe", lambda e, w3=w3, g=g, c=c, psm=psm: e.matmul(
                            psm[:, g * 2:g * 2 + 2], lhsT=w3[:, c, g * 128:(g + 1) * 128],
                            rhs=sc2[:, c * 2:c * 2 + 2], start=(c == 0 and g == 0), stop=(c == 15),
                            skip_group_check=True), [wsl, sc2], [psm])
                P.op("dve", lambda e, psm=psm, l=l, s_=s_: e.tensor_tensor(
                    out=modt[:, l, 4 * s_:4 * s_ + 4].rearrange("p (g j) -> p g j", j=2),
                    in0=psm[:, 0:4].rearrange("p (g j) -> p g j", j=2),
                    in1=pfm[:, l, 2 * s_:2 * s_ + 2].unsqueeze(2).to_broadcast([128, 2, 2]), op=ALU.add),
                    [psm, pfm], [modt])

            def ada_finish(l, part):
                m3 = modt[:, l, :].rearrange("p (g j) -> p g j", j=2)
                for j in range(2):
                    if part in (0, 2):
                        P.op("dve", lambda e, m3=m3, j=j, l=l: e.scalar_tensor_tensor(
                            out=Amod[:, l, j, :], in0=m3[:, 16:32, j], scalar=1.0, in1=pfm[:, l, 48:64],
                            op0=ALU.add, op1=ALU.mult), [modt, pfm], [Amod])
                        P.op("dve", lambda e, m3=m3, j=j, l=l: e.tensor_copy(out=Shm[:, l, j, :], in_=m3[:, 0:16, j]),
                             [modt], [Shm])
                    if part in (1, 2):
                        P.op("dve", lambda e, m3=m3, j=j, l=l: e.tensor_tensor(
                            out=Gtm[:, l, j, :], in0=m3[:, 32:48, j], in1=pfm[:, l, 64:80], op=ALU.mult), [modt, pfm], [Gtm])

            ws_begin([wada_d[0, s_] for s_ in range(16)])
            for s_ in range(16):
                ada_slot(0, s_)
            ada_finish(0, 0)

            ck(1, [(modt[:, 0, :], 96), (modt[:, 1, :], 96), (Amod[:, 0, 0, :], 16), (Shm[:, 0, 0, :], 16), (Gtm[:, 0, 0, :], 16)])
            cnt = {"st": 0, "ta": 0, "tb": 0, "tbf": 0, "fb": 0, "hb": 0, "os": 0, "yb": 0, "ev": 0}

            def rot(name, lst):
                i = cnt[name] % len(lst)
                cnt[name] += 1
                return lst[i]

            def rstd_from_ss(ssap, sstile, n, width, out_tile):
                P.op("act", lambda e: e.activation(out=out_tile[:, 8:8 + width], in_=ssap, func=AF.Ln,
                                                   scale=1.0 / n, bias=EPS), [sstile], [out_tile])
                P.op("act", lambda e: e.activation(out=out_tile[:, 0:width], in_=out_tile[:, 8:8 + width],
                                                   func=AF.Exp, scale=-0.5), [out_tile], [out_tile])

            def make_hT(l, tau, xt):
                var = 1 if tau < 2 else 0
                s_ = rot("st", st)
                xn = W[2]
                P.op("act", lambda e: e.activation(out=xn[:, 0:D], in_=xt[:, :], func=AF.Square,
                                                   accum_out=s_[:, 4:5]), [xt], [xn, s_])
                rstd_from_ss(s_[:, 4:5], s_, D, 1, s_)
                P.op("dve", lambda e: e.tensor_scalar(out=xn[:, 0:D], in0=xt[:, :], scalar1=s_[:, 0:1], scalar2=None,
                                                      op0=ALU.mult), [xt, s_], [xn])
                for half in range(2):
                    hb = rot("hb", HB)
                    for c8 in range(8):
                        c = half * 8 + c8
                        P.op("pe", lambda e, hb=hb, c=c, c8=c8: e.transpose(
                            hb[:, c8 * 128:(c8 + 1) * 128], xn[:, c * 128:(c + 1) * 128], ident_b), [xn, cb], [hb])
                    for c8 in range(8):
                        c = half * 8 + c8
                        if half == 0:
                            P.op("act", lambda e, hb=hb, c=c, c8=c8: e.activation(
                                out=hT[:, c, tau * 128:(tau + 1) * 128], in_=hb[:, c8 * 128:(c8 + 1) * 128],
                                func=AF.Identity, scale=Amod[:, l, var, c:c + 1], bias=Shm[:, l, var, c:c + 1]),
                                [hb, Amod, Shm], [hT_k[tau]])
                        else:
                            P.op("dve", lambda e, hb=hb, c=c, c8=c8: e.tensor_scalar(
                                out=hT[:, c, tau * 128:(tau + 1) * 128], in0=hb[:, c8 * 128:(c8 + 1) * 128],
                                scalar1=Amod[:, l, var, c:c + 1], scalar2=Shm[:, l, var, c:c + 1],
                                op0=ALU.mult, op1=ALU.add), [hb, Amod, Shm], [hT_k[tau]])

            def x_src(tau):
                return ctx_d[tau * 128:(tau + 1) * 128, :] if tau < 2 else x_d[(tau - 2) * 128:(tau - 1) * 128, :]

            for tau in range(NTILE):
                xt = R[0]
                dma_in(xt[:, :], x_src(tau), [xt])
                make_hT(0, tau, xt)

            ck(2, [(hT[:, 0, 0:512], 512), (hT[:, 15, 1792:2304], 512)])
            CHUNKS = [(0, 256, [0, 1])] + [(256 + 512 * i, 512, [2 + 4 * i + j for j in range(4)]) for i in range(4)]

            def proj_fm(wsl, coff, c0, n):
                fb = FB[cnt["fb"] % 2]
                cnt["fb"] += 1
                w3 = wsl[:, :].rearrange("p (c n) -> p c n", n=256)
                tiles = range(c0 // 128, (c0 + n) // 128)
                for c in range(16):
                    P.op("pe", lambda e, fb=fb, w3=w3, c=c: e.matmul(
                        fb[:, 0:n], lhsT=w3[:, c, coff:coff + 128], rhs=hT[:, c, c0:c0 + n],
                        start=(c == 0), stop=(c == 15)), [wsl] + [hT_k[t] for t in tiles], [fb])
                return fb

            def proj_tm(wsl, tau, fb, off):
                w3 = wsl[:, :].rearrange("p (c n) -> p c n", n=256)
                for c in range(16):
                    P.op("pe", lambda e, w3=w3, c=c: e.matmul(
                        fb[:, off:off + 256], lhsT=hT[:, c, tau * 128:(tau + 1) * 128], rhs=w3[:, c, :],
                        start=(c == 0), stop=(c == 15)), [wsl, hT_k[tau]], [fb])

            def store_y(l, src_ap, src_buf, tau, col0, ncol):
                P.dma("sp", yscr_d[tau * 128:(tau + 1) * 128, col0:col0 + ncol], src_ap, reads=[src_buf], writes=[yscr_k])

            for l in range(DEPTH):
                last = (l == DEPTH - 1)
                lam_init = 0.8 - 0.6 * math.exp(-0.3 * l)
                seg = []
                for h_ in range(8):
                    if l == 0:
                        seg += [win_d[l, 2 * h_], wada_d[1, 3 * h_], wada_d[0, 16 + h_], win_d[l, 2 * h_ + 1],
                                wada_d[1, 3 * h_ + 1], wada_d[1, 3 * h_ + 2]]
                    else:
                        seg += [win_d[l, 2 * h_], win_d[l, 2 * h_ + 1]]
                for hd_ in range(4):
                    seg += [win_d[l, 16 + 4 * hd_ + j_] for j_ in range(4)]
                ws_begin(seg)
                dma_in(R[0][:, :], rope_d[0], [R[0]])
                dma_in(R[1][:, :], rope_d[1], [R[1]])
                P.dma("pool", wg[:, :], wgate_d[l], writes=[wg])
                pl = prow[:, l, :]
                P.op("dve", lambda e, pl=pl: e.tensor_tensor(out=tmpA[0][:, 0:64], in0=pl[:, 144:208], in1=pl[:, 208:272],
                                                            op=ALU.mult), [prow], [tmpA[0]])
                P.op("dve", lambda e, pl=pl: e.tensor_tensor(out=tmpA[0][:, 64:128], in0=pl[:, 272:336], in1=pl[:, 336:400],
                                                            op=ALU.mult), [prow], [tmpA[0]])
                P.op("dve", lambda e: e.reduce_sum(out=lamt[:, 2:3], in_=tmpA[0][:, 0:64], axis=AX.X), [tmpA[0]], [lamt])
                P.op("dve", lambda e: e.reduce_sum(out=lamt[:, 3:4], in_=tmpA[0][:, 64:128], axis=AX.X), [tmpA[0]], [lamt])
                P.op("act", lambda e: e.activation(out=lamt[:, 4:6], in_=lamt[:, 2:4], func=AF.Exp), [lamt], [lamt])
                P.op("dve", lambda e, li=lam_init: e.scalar_tensor_tensor(
                    out=lamt[:, 0:1], in0=lamt[:, 4:5], scalar=li, in1=lamt[:, 5:6], op0=ALU.add, op1=ALU.subtract),
                    [lamt], [lamt])
                P.op("dve", lambda e: e.tensor_scalar(out=lamt[:, 1:2], in0=lamt[:, 0:1], scalar1=-1.0, scalar2=None,
                                                      op0=ALU.mult), [lamt], [lamt])
                P.op("dve", lambda e, pl=pl, li=lam_init: e.tensor_scalar(
                    out=subw[:, :], in0=pl[:, 0:128], scalar1=1.0 - li, scalar2=None, op0=ALU.mult), [prow], [subw])

                for h in range(8):
                    wqk = ws_next()
                    P.op("pool", lambda e: e.memset(VaA[:, :, 128:129], 1.0), [], [G[1]])
                    items = [(ci_, c0, n, part) for ci_, (c0, n, tiles) in enumerate(CHUNKS) for part in range(2)]
                    pbanks = [FB[0], FB[1], FB[6]]
                    stA = {}

                    def stage_A(i):
                        ci_, c0, n, part = items[i]
                        dst = qT if part == 0 else kT
                        fb = pbanks[i % 3]
                        w3 = wqk[:, :].rearrange("p (c n) -> p c n", n=256)
                        tls = range(c0 // 128, (c0 + n) // 128)
                        for c in range(16):
                            P.op("pe", lambda e, fb=fb, w3=w3, c=c, c0=c0, n=n, part=part: e.matmul(
                                fb[:, 0:n], lhsT=w3[:, c, part * 128:(part + 1) * 128], rhs=hT[:, c, c0:c0 + n],
                                start=(c == 0), stop=(c == 15)), [wqk] + [hT_k[t] for t in tls], [fb])
                        if c0 < CTX:
                            P.op("act", lambda e, fb=fb, dst=dst, c0=c0, n=n: e.activation(
                                out=dst[:, c0:c0 + n], in_=fb[:, 0:n], func=AF.Identity), [fb], [qk_k[part][ci_]])
                        else:
                            tb_i = 2 + (i % 2)
                            P.op("act", lambda e, fb=fb, tb_i=tb_i, n=n: e.activation(
                                out=PTb[tb_i][:, 0:n], in_=fb[:, 0:n], func=AF.Identity), [fb], [PT_k[tb_i]])
                            stA[i] = (fb, tb_i)

                    def stage_B(i):
                        ci_, c0, n, part = items[i]
                        if c0 < CTX:
                            return
                        dst = qT if part == 0 else kT
                        fb, tb_i = stA[i]
                        t0 = c0 - CTX
                        f2 = FB[2] if i % 2 == 0 else FB[7]
                        ta_ = rot("ta", tmpA)
                        tb2 = rot("tb", tmpB)
                        P.op("pe", lambda e, tb_i=tb_i, f2=f2, n=n: e.matmul(
                            f2[:, 0:n], lhsT=perm_b, rhs=PTb[tb_i][:, 0:n], start=True, stop=True), [PT_k[tb_i], cb], [f2])
                        P.op("dve", lambda e, fb=fb, ta_=ta_, t0=t0, n=n: e.tensor_tensor(
                            out=ta_[:, 0:n], in0=fb[:, 0:n], in1=R[0][:, t0:t0 + n], op=ALU.mult), [fb, R[0]], [ta_])
                        P.op("dve", lambda e, f2=f2, tb2=tb2, t0=t0, n=n: e.tensor_tensor(
                            out=tb2[:, 0:n], in0=f2[:, 0:n], in1=R[1][:, t0:t0 + n], op=ALU.mult), [f2, R[1]], [tb2])
                        P.op("dve", lambda e, ta_=ta_, tb2=tb2, dst=dst, c0=c0, n=n: e.tensor_tensor(
                            out=dst[:, c0:c0 + n], in0=ta_[:, 0:n], in1=tb2[:, 0:n], op=ALU.add), [ta_, tb2], [qk_k[part][ci_]])

                    def stage_C1(i):
                        ci_, c0, n, part = items[i]
                        dst = qT if part == 0 else kT
                        tq_i = i % 2
                        P.op("act", lambda e, dst=dst, tq_i=tq_i, c0=c0, n=n: e.activation(
                            out=PTb[tq_i][:, 0:n], in_=dst[:, c0:c0 + n], func=AF.Square), [qk_k[part][ci_]], [PT_k[tq_i]])

                    def stage_C(i):
                        ci_, c0, n, part = items[i]
                        tq_i = i % 2
                        f3 = FB[3]
                        P.op("pe", lambda e, tq_i=tq_i, f3=f3, n=n: e.matmul(
                            f3[:, 0:n], lhsT=bo_b, rhs=PTb[tq_i][:, 0:n], start=True, stop=True), [PT_k[tq_i], cb], [f3])
                        col = part * 5 + ci_
                        P.op("dve", lambda e, f3=f3, col=col, n=n: e.reduce_max(
                            out=qkst[:, col:col + 1], in_=f3[:, 0:n], axis=AX.X), [f3], [qkst])

                    NI = len(items)
                    for i in range(NI + 3):
                        if 0 <= i - 3 < NI:
                            stage_C(i - 3)
                        if 0 <= i - 2 < NI:
                            stage_C1(i - 2)
                        if i < NI:
                            stage_A(i)
                        if 0 <= i - 1 < NI:
                            stage_B(i - 1)
                    P.op("dve", lambda e: e.reduce_max(out=qkst[:, 10:11], in_=qkst[:, 0:5], axis=AX.X), [qkst], [qkst])
                    P.op("dve", lambda e: e.reduce_max(out=qkst[:, 11:12], in_=qkst[:, 5:10], axis=AX.X), [qkst], [qkst])
                    P.op("dve", lambda e: e.tensor_tensor(out=qkst[:, 12:13], in0=qkst[:, 10:11], in1=qkst[:, 11:12],
                                                          op=ALU.mult), [qkst], [qkst])
                    P.op("dve", lambda e: e.tensor_scalar(out=qkst[:, 14:15], in0=bo_b[:, 0:1], scalar1=qkst[:, 12:13],
                                                          scalar2=None, op0=ALU.mult), [qkst, cb], [qkst])
                    P.op("dve", lambda e: e.tensor_scalar(out=qkst[:, 15:16], in0=bo_b[:, 64:65], scalar1=qkst[:, 12:13],
                                                          scalar2=None, op0=ALU.mult), [qkst, cb], [qkst])
                    f3 = FB[3]
                    P.op("pe", lambda e, f3=f3: e.matmul(f3[:, 0:2], lhsT=ones_f, rhs=qkst[:, 14:16],
                                                         start=True, stop=True), [qkst, cf], [f3])
                    P.op("dve", lambda e, f3=f3: e.reduce_max(out=abias[:, 1:2], in_=f3[:, 0:2], axis=AX.X), [f3], [abias])
                    P.op("act", lambda e: e.activation(out=abias[:, 2:3], in_=abias[:, 1:2], func=AF.Ln,
                                                       scale=1.0 / 64.0, bias=1e-30), [abias], [abias])
                    P.op("act", lambda e: e.activation(out=abias[:, 3:4], in_=abias[:, 2:3], func=AF.Exp, scale=0.5),
                         [abias], [abias])
                    P.op("dve", lambda e: e.tensor_scalar(out=abias[:, 0:1], in0=abias[:, 3:4], scalar1=-1.05 / 8.0,
                                                          scalar2=None, op0=ALU.mult), [abias], [abias])
                    if h == 0:
                        ck(3, [(qT[:, 0:512], 512), (kT[:, 0:512], 512), (qT[:, 1792:2304], 512), (kT[:, 1792:2304], 512), (abias[:, 0:4], 4)])
                    if l == 0:
                        ada_slot(1, 3 * h)
                        ada_slot(0, 16 + h)
                    wvg = ws_next()
                    for t2 in range(0, NTILE, 2):
                        fb = FB[4 + (t2 // 2) % 2]
                        for j in range(2):
                            proj_tm(wvg, t2 + j, fb, j * 256)
                        f3v = fb[:, :].rearrange("p (j c) -> p j c", c=256)
                        P.op("dve", lambda e, f3v=f3v, t2=t2: e.tensor_copy(
                            out=VaA[:, t2:t2 + 2, 0:128], in_=f3v[:, :, 0:128]), [fb], [G[1]])
                        P.op("act", lambda e, f3v=f3v, t2=t2: e.activation(
                            out=sgA[:, t2:t2 + 2, :], in_=f3v[:, :, 128:256], func=AF.Silu), [fb], [G[1]])
                    P.op("pool", lambda e: e.tensor_tensor(
                        out=sgA[:, :, :], in0=sgA[:, :, :], in1=subw[:, :].unsqueeze(1).to_broadcast([128, NTILE, 128]),
                        op=ALU.mult), [G[1], subw], [G[1]])

                    if h == 0:
                        ck(4, [(VaA[:, 0, :], 129), (sgA[:, 0, :], 128), (VaA[:, 17, :], 129), (sgA[:, 17, :], 128)])
                    if l == 0:
                        ada_slot(1, 3 * h + 1)
                        ada_slot(1, 3 * h + 2)
                    def emit_S(ch, ki):
                        q0, qn, ktiles, tq_tiles = ch
                        b = ki % 2
                        kt = ktiles[ki]
                        for sub in range(2):
                            fs = FB[0 + sub] if b == 0 else FB[6 + sub]
                            P.op("pe", lambda e, fs=fs, sub=sub, kt=kt, q0=q0, qn=qn: e.matmul(
                                fs[:, 0:qn], lhsT=kT[sub * 64:(sub + 1) * 64, kt * 128:(kt + 1) * 128],
                                rhs=qT[sub * 64:(sub + 1) * 64, q0:q0 + qn], start=True, stop=True),
                                [qk_k[1][0 if kt < 2 else 1 + (kt - 2) // 4], qk_k[0][0 if q0 == 0 else 1 + (q0 - 256) // 512]], [fs])

                    def emit_exp(ch, ki):
                        q0, qn, ktiles, tq_tiles = ch
                        b = ki % 2
                        pt, ptk = PTb[b], PT_k[b]
                        sc3 = SC2[b][:, :].rearrange("p (s c) -> p s c", c=512)
                        pt3 = pt.rearrange("p (s c) -> p s c", c=512)
                        P.op("act", lambda e, sc3=sc3, pt3=pt3, qn=qn: e.activation(
                            out=pt3[:, :, 0:qn], in_=sc3[:, :, 0:qn], func=AF.Exp, scale=0.125, bias=abias[:, 0:1]),
                            [SC2_k[b][0], SC2_k[b][1], abias], [ptk])

                    def emit_PV(ch, ki):
                        q0, qn, ktiles, tq_tiles = ch
                        nj = qn // 128
                        nk = len(ktiles)
                        kt = ktiles[ki]
                        pt, ptk = PTb[ki % 2], PT_k[ki % 2]
                        for sub in range(2):
                            for j in range(nj):
                                P.op("pe", lambda e, pt=pt, j=j, sub=sub, kt=kt, ki=ki, nk=nk: e.matmul(
                                    FB[2 + j][:, sub * 129:(sub + 1) * 129],
                                    lhsT=pt[:, sub * 512 + j * 128:sub * 512 + (j + 1) * 128],
                                    rhs=VaA[:, kt, :], start=(ki == 0 and sub == 0), stop=(ki == nk - 1),
                                    skip_group_check=True), [ptk, G[1]], [FB[2 + j]])

                    fin_state = {}

                    def fin1(ch):
                        q0, qn, ktiles, tq_tiles = ch
                        nj = qn // 128
                        os_ = rot("os", osb)
                        s_ = rot("st", st)
                        a3s = [FB[2 + j][:, 0:258].rearrange("p (s c) -> p s c", c=129) for j in range(nj)]
                        tA = [tmpA[0], tmpA[1]]
                        for j in range(nj):
                            a3 = a3s[j]
                            P.op("dve", lambda e, a3=a3, j=j, s_=s_: e.reciprocal(out=s_[:, 8 + 2 * j:10 + 2 * j], in_=a3[:, :, 128]),
                                 [FB[2 + j]], [s_])
                            P.op("dve", lambda e, j=j, s_=s_: e.tensor_tensor(
                                out=s_[:, 9 + 2 * j:10 + 2 * j], in0=s_[:, 9 + 2 * j:10 + 2 * j], in1=lamt[:, 1:2],
                                op=ALU.mult), [s_, lamt], [s_])
                        for j in range(nj):
                            a3 = a3s[j]
                            ta_ = tA[j // 2]
                            P.op("act", lambda e, a3=a3, j=j, ta_=ta_, s_=s_: e.activation(
                                out=ta_[:, (j % 2) * 128:(j % 2) * 128 + 128], in_=a3[:, 0, 0:128], func=AF.Identity,
                                scale=s_[:, 8 + 2 * j:9 + 2 * j]), [FB[2 + j], s_], [ta_])
                            P.op("dve", lambda e, a3=a3, j=j, ta_=ta_, s_=s_, os_=os_: e.scalar_tensor_tensor(
                                out=os_[:, j, :], in0=a3[:, 1, 0:128], scalar=s_[:, 9 + 2 * j:10 + 2 * j],
                                in1=ta_[:, (j % 2) * 128:(j % 2) * 128 + 128],
                                op0=ALU.mult, op1=ALU.add), [FB[2 + j], s_, ta_], [os_])
                        fin_state["p"] = (ch, os_, s_)

                    def fin2():
                        if "p" not in fin_state:
                            return
                        ch, os_, s_ = fin_state.pop("p")
                        q0, qn, ktiles, tq_tiles = ch
                        nj = qn // 128
                        yb = rot("yb", ybuf)
                        for j in range(nj):
                            tb2 = rot("tb", tmpB)
                            P.op("act", lambda e, j=j, tb2=tb2, os_=os_, s_=s_: e.activation(
                                out=tb2[:, 0:128], in_=os_[:, j, :], func=AF.Square, accum_out=s_[:, 4 + j:5 + j]),
                                [os_], [tb2, s_])
                        s2 = rot("st", st)
                        P.op("act", lambda e, s_=s_, s2=s2, nj=nj: e.activation(out=s2[:, 8:8 + nj], in_=s_[:, 4:4 + nj], func=AF.Ln,
                                                                                 scale=1.0 / 128, bias=EPS), [s_], [s2])
                        P.op("act", lambda e, s2=s2, nj=nj: e.activation(out=s2[:, 0:nj], in_=s2[:, 8:8 + nj], func=AF.Exp,
                                                                         scale=-0.5), [s2], [s2])
                        for j in range(nj):
                            tau = tq_tiles[j]
                            P.op("dve", lambda e, j=j, tau=tau, os_=os_, s2=s2, yb=yb: e.scalar_tensor_tensor(
                                out=yb[:, j, 0:128], in0=os_[:, j, :], scalar=s2[:, j:j + 1], in1=sgA[:, tau, :],
                                op0=ALU.mult, op1=ALU.mult), [os_, s2, G[1]], [yb])
                            store_y(l, yb[:, j, 0:128], yb, tau, h * 128, 128)

                    chunks = []
                    if not last:
                        chunks.append((0, 256, [0, 1], [0, 1]))
                    for qc in range(4):
                        chunks.append((256 + 512 * qc, 512, list(range(NTILE)), [2 + 4 * qc + j for j in range(4)]))
                    for ci_c, ch in enumerate(chunks):
                        nk = len(ch[2])
                        if ci_c == 0:
                            emit_S(ch, 0)
                        for ki in range(nk):
                            if ki + 1 < nk:
                                emit_S(ch, ki + 1)
                            if not (ki == 0 and ci_c > 0):
                                emit_exp(ch, ki)
                            emit_PV(ch, ki)
                            if ki == 1:
                                fin2()
                        if ci_c + 1 < len(chunks):
                            nx = chunks[ci_c + 1]
                            emit_S(nx, 0)
                            emit_exp(nx, 0)
                        fin1(ch)
                    fin2()
                    if l == 0 and h == 7:
                        ada_finish(1, 2)
                        ada_finish(0, 1)

                    if h == 0:
                        ck(5, [(yscr_d[0:128, 0:128], 128), (yscr_d[256:384, 0:128], 128), (yscr_d[2176:2304, 0:128], 128)])
                for i8 in range(8):
                    P.dma("pool", woutb_d[l, :, i8 * 4096:(i8 + 1) * 4096], wout_d[l, :, i8 * 4096:(i8 + 1) * 4096],
                          writes=[woutb_k[l]])
                fg = FB[0]
                for tau in range(NTILE):
                    w3g = wg[:, :].rearrange("p (c n) -> p c n", n=16)
                    for c in range(16):
                        P.op("pe", lambda e, tau=tau, c=c, w3g=w3g: e.matmul(
                            fg[:, tau * 16:(tau + 1) * 16], lhsT=hT[:, c, tau * 128:(tau + 1) * 128], rhs=w3g[:, c, :],
                            start=(c == 0), stop=(c == 15)), [wg, hT_k[tau]], [fg])
                fg3 = fg[:, 0:NTILE * 16].rearrange("p (t c) -> p t c", c=16)
                P.op("dve", lambda e, fg3=fg3, l=l: e.tensor_tensor(
                    out=gz[:, :, :], in0=fg3, in1=prow[:, l, 128:144].unsqueeze(1).to_broadcast([128, NTILE, 16]),
                    op=ALU.add), [fg, prow], [gz])
                P.op("act", lambda e: e.activation(out=gz[:, :, :], in_=gz[:, :, :], func=AF.Tanh, scale=1.0 / 15.0),
                     [gz], [gz])
                P.op("dve", lambda e: e.tensor_scalar(out=gig[:, :, :], in0=gz[:, :, 0:8], scalar1=15.0, scalar2=LN_KS,
                                                      op0=ALU.mult, op1=ALU.add), [gz], [gig])
                P.op("act", lambda e: e.activation(out=glf[:, :, :], in_=gz[:, :, 8:16], func=AF.Exp, scale=-15.0),
                     [gz], [glf])
                P.op("act", lambda e: e.activation(out=glf[:, :, :], in_=glf[:, :, :], func=AF.Ln, bias=1.0),
                     [glf], [glf])
                P.op("dve", lambda e: e.tensor_scalar(out=glf[:, :, :], in0=glf[:, :, :], scalar1=-1.0, scalar2=None,
                                                      op0=ALU.mult), [glf], [glf])
                fc = FB[1]
                for tau in range(NTILE):
                    for dr in range(2):
                        P.op("pe", lambda e, tau=tau, dr=dr: e.matmul(
                            fc[:, tau * 16 + dr * 4:tau * 16 + dr * 4 + 4], lhsT=tri[dr],
                            rhs=glf[:, tau, dr * 4:dr * 4 + 4], start=True, stop=True), [cf, glf], [fc])
                    P.op("pe", lambda e, tau=tau: e.matmul(
                        fc[:, tau * 16 + 8:tau * 16 + 16], lhsT=ones_f, rhs=glf[:, tau, :], start=True, stop=True),
                        [cf, glf], [fc])
                fc3 = fc[:, 0:NTILE * 16].rearrange("p (t c) -> p t c", c=16)
                P.op("dve", lambda e, fc3=fc3: e.tensor_copy(out=gcum[:, :, :], in_=fc3), [fc], [gcum])
                P.op("dve", lambda e: e.tensor_tensor(out=ga1[:, :, :], in0=gig[:, :, :], in1=gcum[:, :, 0:8],
                                                      op=ALU.subtract), [gig, gcum], [ga1])
                P.op("dve", lambda e: e.tensor_tensor(out=gwk[:, :, :], in0=ga1[:, :, :], in1=gcum[:, :, 8:16],
                                                      op=ALU.add), [ga1, gcum], [gwk])
                P.op("act", lambda e: e.activation(out=gwk[:, :, :], in_=gwk[:, :, :], func=AF.Exp), [gwk], [gwk])
                P.op("act", lambda e: e.activation(out=gdec[:, :, :], in_=gcum[:, :, 8:16], func=AF.Exp), [gcum], [gdec])
                P.op("act", lambda e: e.activation(out=geb[:, :, :], in_=gcum[:, :, 0:8], func=AF.Exp), [gcum], [geb])
                P.op("act", lambda e: e.activation(out=gw1[:, :, :], in_=ga1[:, :, :], func=AF.Exp), [ga1], [gw1])

                ck(6, [(glf[:, :, :], 144), (gig[:, :, :], 144), (gcum[:, :, :], 288), (gwk[:, :, :], 144)])
                for hd in range(4):
                    wqk = ws_next()
                    P.dma("sp", mnorm[:, :], mnorm_d[l, hd].partition_broadcast(128), writes=[mnorm])
                    for part in range(2):
                        dst = qT if part == 0 else kT
                        P.op("pool", lambda e: e.memset(xs[:, 0:2320], 0.0), [], [G[3]])
                        for (c0, n, tiles) in CHUNKS:
                            fb = proj_fm(wqk, part * 128, c0, n)
                            so = 1 + c0 if c0 < CTX else 3 + c0
                            P.op("act", lambda e, fb=fb, so=so, n=n: e.activation(
                                out=xs[:, so:so + n], in_=fb[:, 0:n], func=AF.Identity), [fb], [G[3]])
                        ci = part * 4 + hd
                        cw = lambda j, ci=ci, l=l: pfm[:, l, 80 + ci * 3 + j:80 + ci * 3 + j + 1]
                        cbias = pfm[:, l, 104 + ci:105 + ci]
                        P.op("dve", lambda e, cw=cw, cbias=cbias: e.tensor_scalar(
                            out=ys[:, 1:2307], in0=xs[:, 1:2307], scalar1=cw(1), scalar2=cbias, op0=ALU.mult, op1=ALU.add),
                            [G[3], pfm], [G[4]])
                        P.op("dve", lambda e, cw=cw: e.scalar_tensor_tensor(
                            out=ys[:, 1:2307], in0=xs[:, 0:2306], scalar=cw(0), in1=ys[:, 1:2307], op0=ALU.mult, op1=ALU.add),
                            [G[3], G[4], pfm], [G[4]])
                        P.op("dve", lambda e, cw=cw: e.scalar_tensor_tensor(
                            out=ys[:, 1:2307], in0=xs[:, 2:2308], scalar=cw(2), in1=ys[:, 1:2307], op0=ALU.mult, op1=ALU.add),
                            [G[3], G[4], pfm], [G[4]])
                        P.op("act", lambda e, dst=dst: e.activation(out=dst[:, 0:CTX], in_=ys[:, 1:257], func=AF.Silu),
                             [G[4]], G0_all)
                        P.op("act", lambda e, dst=dst: e.activation(out=dst[:, CTX:NT], in_=ys[:, 259:2307], func=AF.Silu),
                             [G[4]], G0_all)
                    P.op("pool", lambda e: e.memset(VaM[:, :, 256:257], 1.0), [], [G[1]])
                    wv = ws_next()
                    for t2 in range(0, NTILE, 2):
                        fb = FB[4 + (t2 // 2) % 2]
                        for j in range(2):
                            proj_tm(wv, t2 + j, fb, j * 256)
                        f3v = fb[:, :].rearrange("p (j c) -> p j c", c=256)
                        P.op("dve", lambda e, f3v=f3v, t2=t2: e.tensor_copy(out=VaM[:, t2:t2 + 2, 0:256], in_=f3v),
                             [fb], [G[1]])
                    wo = ws_next()
                    for t2 in range(0, NTILE, 2):
                        fb = FB[4 + (t2 // 2) % 2]
                        for j in range(2):
                            proj_tm(wo, t2 + j, fb, j * 256)
                        f3v = fb[:, :].rearrange("p (j c) -> p j c", c=256)
                        P.op("act", lambda e, f3v=f3v, t2=t2: e.activation(out=ogM[:, t2:t2 + 2, :], in_=f3v, func=AF.Sigmoid),
                             [fb], [G[2]])
                    wgg = ws_next()
                    for t2 in range(0, NTILE, 2):
                        fb = FB[4 + (t2 // 2) % 2]
                        for j in range(2):
                            proj_tm(wgg, t2 + j, fb, j * 256)
                        f3v = fb[:, :].rearrange("p (j c) -> p j c", c=256)
                        tb_ = rot("tbf", tbf)
                        tb3 = tb_[:, :].rearrange("p (j c) -> p j c", c=256)
                        P.op("act", lambda e, f3v=f3v, tb3=tb3: e.activation(out=tb3, in_=f3v, func=AF.Silu), [fb], [tb_])
                        P.op("pool", lambda e, tb3=tb3, t2=t2: e.tensor_tensor(
                            out=ogM[:, t2:t2 + 2, :], in0=ogM[:, t2:t2 + 2, :], in1=tb3, op=ALU.mult), [tb_, G[2]], [G[2]])

                    orders = [list(range(NTILE)), [1, 0] + list(range(NTILE - 1, 1, -1))]
                    pos = [{t: i for i, t in enumerate(o)} for o in orders]
                    mask01 = [cb[:, 128:256], cb[:, 256:384]]
                    LEAD = 2

                    def stage_state(dr, s_i):
                        tau = orders[dr][s_i]
                        gi = dr * 4 + hd
                        tsl = slice(tau * 128, (tau + 1) * 128)
                        hbk = HB[dr]
                        bU = FB[6 + dr]
                        yield P.op("pe", lambda e, hbk=hbk, tsl=tsl: e.transpose(hbk[:, 640:768], kT[:, tsl], ident_b),
                             G0_all + [cb], [hbk])
                        kw_ = kwb[dr * 2 + s_i % 2]
                        yield P.op("act", lambda e, hbk=hbk, kw_=kw_, tau=tau, gi=gi: e.activation(
                            out=kw_[:, :], in_=hbk[:, 640:768], func=AF.Identity, scale=gwk[:, tau, gi:gi + 1]),
                            [hbk, gwk], [kw_])
                        yield P.op("pe", lambda e, bU=bU, kw_=kw_, tau=tau: e.matmul(
                            bU[:, 0:257], lhsT=kw_[:, :], rhs=VaM[:, tau, :], start=True, stop=True), [kw_, G[1]], [bU])
                        if s_i == 0:
                            yield P.op("dve", lambda e, bU=bU, dr=dr: e.tensor_copy(out=Cst[dr][:, :], in_=bU[:, 0:257]),
                                 [bU], [Cst[dr]])
                        else:
                            yield P.op("dve", lambda e, bU=bU, dr=dr, tau=tau, gi=gi: e.scalar_tensor_tensor(
                                out=Cst[dr][:, :], in0=Cst[dr][:, :], scalar=gdec[:, tau, gi:gi + 1], in1=bU[:, 0:257],
                                op0=ALU.mult, op1=ALU.add), [Cst[dr], gdec, bU], [Cst[dr]])
                        slot = Cring[dr][s_i % RING]
                        yield P.op("act", lambda e, dr=dr, slot=slot: e.activation(out=slot[:, :], in_=Cst[dr][:, :],
                                                                             func=AF.Identity), [Cst[dr]], [slot])

                    def stage_out(dr, s_i):
                        tau = orders[dr][s_i]
                        if last and tau < 2:
                            return
                        gi = dr * 4 + hd
                        tsl = slice(tau * 128, (tau + 1) * 128)
                        bS = FB[dr]
                        yield P.op("pe", lambda e, bS=bS, tsl=tsl: e.matmul(
                            bS[:, 0:128], lhsT=kT[:, tsl], rhs=qT[:, tsl], start=True, stop=True), G0_all, [bS])
                        sw_ = SWT[dr * 2 + s_i % 2]
                        yield P.op("dve", lambda e, bS=bS, sw_=sw_, tau=tau, gi=gi, dr=dr: e.scalar_tensor_tensor(
                            out=sw_[:, :], in0=bS[:, 0:128], scalar=gw1[:, tau, gi:gi + 1], in1=mask01[dr],
                            op0=ALU.mult, op1=ALU.mult), [bS, gw1, cb], [sw_])
                        bH = FB[2 + 2 * dr + s_i % 2]
                        yield P.op("pe", lambda e, bH=bH, sw_=sw_, tau=tau, s_i=s_i: e.matmul(
                            bH[:, 0:257], lhsT=sw_[:, :], rhs=VaM[:, tau, :], start=True, stop=(s_i == 0)),
                            [sw_, G[1]], [bH])
                        if s_i > 0:
                            slot = Cring[dr][(s_i - 1) % RING]
                            yield P.op("pe", lambda e, bH=bH, tsl=tsl, slot=slot: e.matmul(
                                bH[:, 0:257], lhsT=qT[:, tsl], rhs=slot[:, :], start=False, stop=True),
                                G0_all + [slot], [bH])
                        nm_ = numb[dr * 2 + s_i % 2]
                        yield P.op("act", lambda e, bH=bH, nm_=nm_, tau=tau, gi=gi: e.activation(
                            out=nm_[:, :], in_=bH[:, 0:257], func=AF.Identity, scale=geb[:, tau, gi:gi + 1]),
                            [bH, geb], [nm_])
                        s_ = rot("st", st)
                        yield P.op("act", lambda e, nm_=nm_, s_=s_: e.activation(out=s_[:, 0:1], in_=nm_[:, 256:257], func=AF.Abs),
                             [nm_], [s_])
                        yield P.op("dve", lambda e, s_=s_: e.tensor_scalar(
                            out=s_[:, 1:2], in0=s_[:, 0:1], scalar1=1.0, scalar2=None, op0=ALU.max), [s_], [s_])
                        yield P.op("dve", lambda e, s_=s_: e.reciprocal(out=s_[:, 2:3], in_=s_[:, 1:2]), [s_], [s_])
                        second = pos[dr][tau] > pos[1 - dr][tau]
                        if not second:
                            yield P.op("act", lambda e, nm_=nm_, s_=s_, tau=tau: e.activation(
                                out=hpart[:, tau, :], in_=nm_[:, 0:256], func=AF.Identity, scale=s_[:, 2:3]),
                                [nm_, s_], [G[3]])
                        else:
                            hm = hsum[dr]
                            yield P.op("dve", lambda e, nm_=nm_, s_=s_, tau=tau, hm=hm: e.scalar_tensor_tensor(
                                out=hm[:, :], in0=nm_[:, 0:256], scalar=s_[:, 2:3], in1=hpart[:, tau, :],
                                op0=ALU.mult, op1=ALU.add), [nm_, s_, G[3]], [hm])
                            tb2 = rot("tb", tmpB)
                            s3 = rot("st", st)
                            yield P.op("act", lambda e, hm=hm, tb2=tb2, s3=s3: e.activation(
                                out=tb2[:, 0:256], in_=hm[:, :], func=AF.Square, accum_out=s3[:, 4:5]),
                                [hm], [tb2, s3])
                            rstd_from_ss(s3[:, 4:5], s3, 256, 1, s3)
                            yield P.op("dve", lambda e, hm=hm, s3=s3: e.scalar_tensor_tensor(
                                out=hm[:, :], in0=hm[:, :], scalar=s3[:, 0:1], in1=mnorm[:, :], op0=ALU.mult,
                                op1=ALU.mult), [hm, s3, mnorm], [hm])
                            yb = rot("yb", ybuf)
                            yield P.op("dve", lambda e, hm=hm, yb=yb, tau=tau: e.tensor_tensor(
                                out=yb[:, 0, :], in0=hm[:, :], in1=ogM[:, tau, :], op=ALU.mult), [hm, G[2]], [yb])
                            store_y(l, yb[:, 0, :], yb, tau, 1024 + hd * 256, 256)

                    def run_rr(gens):
                        gens = list(gens)
                        while gens:
                            nxt = []
                            for g_ in gens:
                                try:
                                    next(g_)
                                    nxt.append(g_)
                                except StopIteration:
                                    pass
                            gens = nxt

                    for it in range(-LEAD, NTILE):
                        gens = []
                        for dr in range(2):
                            sa = it + LEAD
                            if 0 <= sa <= NTILE - 2:
                                gens.append(stage_state(dr, sa))
                        for dr in range(2):
                            if it >= 0:
                                gens.append(stage_out(dr, it))
                        run_rr(gens)
                ck(8, [(yscr_d[0:128, 1024:1536], 512), (yscr_d[256:384, 1024:1536], 512), (yscr_d[2176:2304, 1024:1536], 512), (yscr_d[2176:2304, 0:1024], 1024)])
                P.barrier()
                for i in range(8):
                    P.dma("sp", G[i][:, 0:4096], woutb_d[l, :, i * 4096:(i + 1) * 4096], reads=[woutb_k[l]], writes=[G[i]])
                tiles = list(range(NTILE)) if not last else list(range(2, NTILE))
                xt = R[0]
                yt = W[0]
                yT_ = W[1]
                YB = [VW(FB[4][:, :].bitcast(BF16), k=FB[4].k), VW(FB[5][:, :].bitcast(BF16), k=FB[5].k)]
                Ttmp = [tmpA[0], tmpA[1], tmpB[0], tmpB[1]]
                xn = W[2]

                def gn_build(var):
                    for c in range(16):
                        ta_ = rot("ta", tmpA)
                        P.op("pool", lambda e, ta_=ta_, c=c, var=var, l=l: e.tensor_scalar(
                            out=ta_[:, 0:128], in0=ones_f, scalar1=Gtm[:, l, var, c:c + 1], scalar2=None, op0=ALU.mult),
                            [cf, Gtm], [ta_])
                        fq = HB[c % 2]
                        P.op("pe", lambda e, ta_=ta_, fq=fq: e.matmul(FB[6 + (0 if fq is HB[0] else 1)][:, 0:128],
                                                                      lhsT=ta_[:, 0:128], rhs=ident_f,
                                                                      start=True, stop=True), [ta_, cf], [fq])
                        P.op("act", lambda e, fq=fq, c=c: e.activation(
                            out=GN[:, c * 128:(c + 1) * 128], in_=FB[6 + (0 if fq is HB[0] else 1)][:, 0:128],
                            func=AF.Identity), [fq], [GN])

                yt_k = [Tk(), Tk()]
                yT_k = [Tk(), Tk()]
                P.op("pool", lambda e: e.memset(yt[:, 0:8], 0.0), [], [yt, yt_k[0], yt_k[1]])
                P.op("pool", lambda e: e.memset(yT_[:, 0:8], 0.0), [], [yT_, yT_k[0], yT_k[1]])

                def YLOAD(tau):
                    yoff = (tau % 2) * 2048
                    P.dma("sp", yt[:, yoff:yoff + 2048], yscr_d[tau * 128:(tau + 1) * 128, :], reads=[yscr_k],
                          writes=[yt_k[tau % 2]])

                def E1a(tau):
                    yoff = (tau % 2) * 2048
                    toff = yoff
                    for half in range(2):
                        hb = YB[half]
                        for c8 in range(8):
                            c = half * 8 + c8
                            P.op("pe", lambda e, hb=hb, c=c, c8=c8, yoff=yoff: e.transpose(
                                hb[:, c8 * 128:(c8 + 1) * 128], yt[:, yoff + c * 128:yoff + (c + 1) * 128], ident_b),
                                [yt_k[tau % 2], cb], [hb])
                        if half == 0:
                            P.op("act", lambda e, hb=hb, half=half, toff=toff: e.activation(
                                out=yT_[:, toff + half * 1024:toff + (half + 1) * 1024], in_=hb[:, :], func=AF.Identity),
                                [hb], [yT_k[tau % 2]])
                        else:
                            P.op("dve", lambda e, hb=hb, half=half, toff=toff: e.tensor_copy(
                                out=yT_[:, toff + half * 1024:toff + (half + 1) * 1024], in_=hb[:, :]), [hb], [yT_k[tau % 2]])
                def E1b(tau):
                    toff = (tau % 2) * 2048
                    for cg in range(4):
                        fo = FB[cg]
                        for c in range(16):
                            P.op("pe", lambda e, fo=fo, c=c, cg=cg, toff=toff: e.matmul(
                                fo[:, :], lhsT=yT_[:, toff + c * 128:toff + (c + 1) * 128],
                                rhs=G[c // 2][:, (c % 2) * 2048 + cg * 512:(c % 2) * 2048 + (cg + 1) * 512],
                                start=(c == 0), stop=(c == 15)), [yT_k[tau % 2], G[c // 2]], [fo])

                def E2a(tau):
                    s_ = rot("st", st)
                    for cg in range(4):
                        fo = FB[cg]
                        jk = tbf[cg % 2]
                        P.op("act", lambda e, fo=fo, jk=jk, cg=cg, s_=s_: e.activation(
                            out=jk[:, :], in_=fo[:, :], func=AF.Square, accum_out=s_[:, 4 + cg:5 + cg]), [fo], [jk, s_])
                    P.op("dve", lambda e, s_=s_: e.reduce_sum(out=s_[:, 3:4], in_=s_[:, 4:8], axis=AX.X), [s_], [s_])
                    rstd_from_ss(s_[:, 3:4], s_, D, 1, s_)
                    for cg in range(4):
                        fo = FB[cg]
                        ta_ = Ttmp[cg]
                        P.op("dve", lambda e, fo=fo, ta_=ta_, cg=cg, s_=s_: e.scalar_tensor_tensor(
                            out=ta_[:, :], in0=fo[:, :], scalar=s_[:, 0:1], in1=GN[:, cg * 512:(cg + 1) * 512],
                            op0=ALU.mult, op1=ALU.mult), [fo, s_, GN], [ta_])

                def XLOAD(tau):
                    if l == 0:
                        dma_in(xt[:, :], x_src(tau), [xt])
                    else:
                        dma_in(xt[:, :], x1_d[(tau - 2) * 128:(tau - 1) * 128, :], [xt], r=[x1_k])

                mst = {}

                def E2b(tau):
                    for cg in range(4):
                        ta_ = Ttmp[cg]
                        P.op("pool", lambda e, ta_=ta_, cg=cg: e.tensor_tensor(
                            out=xt[:, cg * 512:(cg + 1) * 512], in0=xt[:, cg * 512:(cg + 1) * 512], in1=ta_[:, :],
                            op=ALU.add), [ta_, xt], [xt])
                    if last:
                        P.dma("sp", out_d[(tau - 2) * 128:(tau - 1) * 128, :], xt[:, :], reads=[xt])
                    else:
                        if tau >= 2:
                            P.dma("sp", x1_d[(tau - 2) * 128:(tau - 1) * 128, :], xt[:, :], reads=[xt], writes=[x1_k])
                        s_ = rot("st", st)
                        P.op("act", lambda e, s_=s_: e.activation(out=xn[:, 0:D], in_=xt[:, :], func=AF.Square,
                                                                  accum_out=s_[:, 4:5]), [xt], [xn, s_])
                        rstd_from_ss(s_[:, 4:5], s_, D, 1, s_)
                        P.op("dve", lambda e, s_=s_: e.tensor_scalar(out=xn[:, 0:D], in0=xt[:, :], scalar1=s_[:, 0:1],
                                                                    scalar2=None, op0=ALU.mult), [xt, s_], [xn])

                def M2(tau):
                    var = 1 if tau < 2 else 0
                    for half in range(2):
                        hb = HB[half]
                        for c8 in range(8):
                            c = half * 8 + c8
                            P.op("pe", lambda e, hb=hb, c=c, c8=c8: e.transpose(
                                hb[:, c8 * 128:(c8 + 1) * 128], xn[:, c * 128:(c + 1) * 128], ident_b), [xn, cb], [hb])
                        for c8 in range(8):
                            c = half * 8 + c8
                            if half == 0:
                                P.op("act", lambda e, hb=hb, c=c, c8=c8, var=var, tau=tau, l=l: e.activation(
                                    out=hT[:, c, tau * 128:(tau + 1) * 128], in_=hb[:, c8 * 128:(c8 + 1) * 128],
                                    func=AF.Identity, scale=Amod[:, l + 1, var, c:c + 1], bias=Shm[:, l + 1, var, c:c + 1]),
                                    [hb, Amod, Shm], [hT_k[tau]])
                            else:
                                P.op("dve", lambda e, hb=hb, c=c, c8=c8, var=var, tau=tau, l=l: e.tensor_scalar(
                                    out=hT[:, c, tau * 128:(tau + 1) * 128], in0=hb[:, c8 * 128:(c8 + 1) * 128],
                                    scalar1=Amod[:, l + 1, var, c:c + 1], scalar2=Shm[:, l + 1, var, c:c + 1],
                                    op0=ALU.mult, op1=ALU.add), [hb, Amod, Shm], [hT_k[tau]])

                cur_var = None
                for i_, tau in enumerate(tiles):
                    var = 1 if tau < 2 else 0
                    if var != cur_var:
                        cur_var = var
                        gn_build(var)
                    if i_ == 0:
                        YLOAD(tau)
                    if i_ + 1 < len(tiles):
                        YLOAD(tiles[i_ + 1])
                    XLOAD(tau)
                    if i_ == 0:
                        E1a(tau)
                    E1b(tau)
                    E2a(tau)
                    if i_ >= 1 and not last:
                        M2(tiles[i_ - 1])
                    if i_ + 1 < len(tiles):
                        E1a(tiles[i_ + 1])
                    E2b(tau)
                if not last:
                    M2(tiles[-1])
                if l == 0:
                    ck(9, [(x1_d[0:128, 0:1024], 1024), (x1_d[1920:2048, 1024:2048], 1024), (hT[:, 0, 0:512], 512), (hT[:, 15, 1792:2304], 512)])
                P.barrier()

        try:
            _body()
        except _Stop:
            pass
        P.finish()
    return nc


def _consts():
    p = np.arange(128)[:, None]
    j = np.arange(128)[None, :]
    c = np.zeros((128, 1024), np.float32)
    c[:, 0:128] = (p == j)
    c[:, 128:256] = (p <= j)
    c[:, 256:384] = (p >= j)
    c[:, 384:512] = 1.0
    c[:, 512:640] = np.where(p <= j, 0.0, NEG)
    c[:, 640:768] = np.where(p >= j, 0.0, NEG)
    c[:, 768:896] = ((p // 64) == (j // 64))
    c[:, 896:1024] = (p == (j ^ 1))
    return c


def _rope():
    t = np.arange(T)
    row = (t // 64).astype(np.float32)
    col = (t % 64).astype(np.float32)
    inv = (np.float32(10000.0) ** (-np.arange(16, dtype=np.float32) / np.float32(16))).astype(np.float32)
    ang = np.concatenate([row[:, None] * inv[None, :], col[:, None] * inv[None, :]], axis=1).astype(np.float32)
    cos = np.cos(ang.astype(np.float64))
    sin = np.sin(ang.astype(np.float64))
    tab = np.zeros((2, 128, T), np.float32)
    for pp in range(128):
        d = pp % 64
        i = d // 2
        tab[0, pp] = cos[:, i]
        tab[1, pp] = -sin[:, i] if d % 2 == 0 else sin[:, i]
    return tab


def _slots(w, idx_list):
    ws = w[:, np.concatenate(idx_list)]
    ns = len(idx_list)
    return np.ascontiguousarray(ws.reshape(16, 128, ns, 256).transpose(2, 1, 0, 3).reshape(ns, 128, 4096))


_NC_CACHE = {}


def _prep(x, c, ctx, c_ctx, w_ada, b_ada, norm_pre, norm_post, w_in, w_out,
          lam_q1, lam_k1, lam_q2, lam_k2, attn_subln, conv_w, conv_b,
          i_bias, f_bias, mlstm_norm, cores=range(8)):
    f = lambda a: np.asarray(a, dtype=np.float32)
    x, c, ctx, c_ctx, w_ada, b_ada, norm_pre, norm_post, w_in, w_out = map(
        f, (x, c, ctx, c_ctx, w_ada, b_ada, norm_pre, norm_post, w_in, w_out))
    lam_q1, lam_k1, lam_q2, lam_k2, attn_subln, conv_w, conv_b, i_bias, f_bias, mlstm_norm = map(
        f, (lam_q1, lam_k1, lam_q2, lam_k2, attn_subln, conv_w, conv_b, i_bias, f_bias, mlstm_norm))
    A = np.arange
    idx = []
    for h in range(8):
        idx.append(np.concatenate([h * 128 + A(128), 1024 + h * 128 + A(128)]))
        idx.append(np.concatenate([2048 + h * 128 + A(128), 3072 + h * 128 + A(128)]))
    for hd in range(4):
        idx.append(np.concatenate([4096 + hd * 128 + A(128), 4608 + hd * 128 + A(128)]))
        idx.append(5120 + hd * 256 + A(256))
        idx.append(6144 + hd * 256 + A(256))
        idx.append(7168 + hd * 256 + A(256))
    win = np.stack([_slots(w_in[l], idx) for l in range(DEPTH)])
    wada = np.stack([_slots(w_ada[l], [s * 256 + A(256) for s in range(NSLOT_ADA)]) for l in range(DEPTH)])
    wgate = np.stack([np.ascontiguousarray(w_in[l][:, 8192:8208].reshape(16, 128, 16).transpose(1, 0, 2).reshape(128, 256))
                      for l in range(DEPTH)])
    wout = np.stack([np.ascontiguousarray(w_out[l].reshape(16, 128, D).transpose(1, 0, 2).reshape(128, 16 * D))
                     for l in range(DEPTH)])
    fm = lambda v, n: np.ascontiguousarray(v.reshape(n, 128).T)
    pfm = np.zeros((DEPTH, 128, 112), np.float32)
    prow = np.zeros((DEPTH, 400), np.float32)
    for l in range(DEPTH):
        pfm[l, :, 0:48] = fm(b_ada[l], 48)
        pfm[l, :, 48:64] = fm(norm_pre[l], 16)
        pfm[l, :, 64:80] = fm(norm_post[l], 16)
        pfm[l, :, 80:104] = conv_w[l].reshape(3, 8, 128).transpose(2, 1, 0).reshape(128, 24)
        pfm[l, :, 104:112] = fm(conv_b[l], 8)
        prow[l, 0:128] = attn_subln[l]
        prow[l, 128:136] = i_bias[l].reshape(8)
        prow[l, 136:144] = f_bias[l].reshape(8)
        prow[l, 144:208] = lam_q1[l]
        prow[l, 208:272] = lam_k1[l]
        prow[l, 272:336] = lam_q2[l]
        prow[l, 336:400] = lam_k2[l]
    mnorm = np.ascontiguousarray(mlstm_norm.reshape(DEPTH, 4, 256))
    cst = _consts()
    rope = _rope()
    in_maps = []
    for b in cores:
        c2 = np.stack([c[b], c_ctx], axis=0)
        c2fm = np.ascontiguousarray(c2.reshape(2, 16, 128).transpose(2, 1, 0).reshape(128, 32))
        in_maps.append({
            "x": np.ascontiguousarray(x[b]), "ctx": np.ascontiguousarray(ctx[b]), "c2fm": c2fm,
            "wada": wada, "win": win, "wgate": wgate, "wout": wout, "pfm": pfm, "prow": prow,
            "mnorm": mnorm, "cst": cst, "rope": rope,
        })
    return in_maps


def kernel(**inputs):
    in_maps = _prep(**inputs)
    if "nc" not in _NC_CACHE:
        _NC_CACHE["nc"] = build_program()
    nc = _NC_CACHE["nc"]
    res = run_bass_kernel_spmd(nc, in_maps, core_ids=list(range(8)))
    return np.stack([np.asarray(r["out"], dtype=np.float32) for r in res.results], axis=0)
```

```python
import math
import numpy as np
import concourse.bass as bass
import concourse.mybir as mybir
from concourse.bass_utils import run_bass_kernel_spmd
from contextlib import ExitStack

F32 = mybir.dt.float32
BF16 = mybir.dt.bfloat16
AF = mybir.ActivationFunctionType
ALU = mybir.AluOpType
AX = mybir.AxisListType

D = 2048
T = 2048
CTX = 256
NT = T + CTX
NTILE = NT // 128
DEPTH = 2
EPS = 1e-6
NSLOT_IN = 32
NSLOT_ADA = 24
LN_KS = math.log(128.0 ** -0.5)
NEG = -30000.0

ENGS = ["pe", "act", "dve", "pool", "sp"]
EPOCH = 12000
NDMA = 24


class Tk:
    __slots__ = ("w", "r", "rd", "ps")

    def __init__(self):
        self.w = None
        self.r = {}
        self.rd = []
        self.ps = False


class Buf:
    def __init__(self, t):
        self.t = t
        self.k = Tk()

    def __getitem__(self, i):
        return self.t[i]


class Op:
    __slots__ = ("eng", "fn", "deps", "signal", "sem", "val", "dma")


class Prog:
    def __init__(self, nc, es):
        self.nc = nc
        self.es = es
        self.q = {e: [] for e in ENGS}
        self.dsem = [es.enter_context(nc.semaphore(f"dq{i}")) for i in range(NDMA)]
        self.dcnt = [0] * NDMA
        self.dlast = [None] * NDMA
        self.drr_sp = 0
        self.drr_pool = 0
        self.nsem = 0

    def _rec(self, o, reads, writes):
        eng = o.eng
        reads = [getattr(t, "k", t) for t in reads]
        writes = [getattr(t, "k", t) for t in writes]
        deps = {}
        raw = set()
        for t in reads:
            if t.w is not None:
                deps[id(t.w)] = t.w
                raw.add(id(t.w))
            if t.ps:
                for re_, r in t.r.items():
                    if re_ != eng:
                        deps[id(r)] = r
        for t in writes:
            if t.w is not None:
                deps[id(t.w)] = t.w
            for r in t.r.values():
                deps[id(r)] = r
            for r in t.rd:
                deps[id(r)] = r
        for d in deps.values():
            if d is o:
                continue
            if (not o.dma) and (not d.dma) and d.eng == eng:
                if eng == "pe":
                    continue
                if id(d) not in raw:
                    continue
            o.deps.append(d)
            d.signal = True
        for t in reads:
            if o.dma:
                t.rd.append(o)
            else:
                t.r[eng] = o
        for t in writes:
            t.w = o
            t.r = {}
            t.rd = []
        self.q[eng].append(o)
        return o

    def op(self, eng, fn, reads=(), writes=()):
        o = Op()
        o.eng = eng
        o.fn = fn
        o.deps = []
        o.signal = False
        o.dma = False
        o.sem = None
        o.val = 0
        return self._rec(o, reads, writes)

    def dma(self, eng, out, in_, reads=(), writes=()):
        o = Op()
        o.eng = eng
        o.fn = lambda e: e.dma_start(out=out, in_=in_)
        o.deps = []
        o.signal = False
        o.dma = True
        half = NDMA // 2
        if eng == "sp":
            j = self.drr_sp
            self.drr_sp = (self.drr_sp + 1) % half
        else:
            j = half + self.drr_pool
            self.drr_pool = (self.drr_pool + 1) % (NDMA - half)
        self.dcnt[j] += 1
        o.sem = self.dsem[j]
        o.val = 16 * self.dcnt[j]
        if self.dlast[j] is not None:
            o.deps.append(self.dlast[j])
        self.dlast[j] = o
        return self._rec(o, reads, writes)

    def barrier(self):
        lasts = []
        for e in ENGS:
            for o in reversed(self.q[e]):
                if o.fn is not None and not o.dma:
                    lasts.append(o)
                    break
        lasts += [d for d in self.dlast if d is not None]
        for e in ENGS:
            o = Op()
            o.eng = e
            o.fn = None
            o.deps = list(lasts)
            o.signal = False
            o.dma = False
            o.sem = None
            o.val = 0
            for d in o.deps:
                d.signal = True
            self.q[e].append(o)

    def finish(self):
        o = Op()
        o.eng = "sp"
        o.fn = None
        o.deps = [d for d in self.dlast if d is not None]
        o.signal = False
        o.dma = False
        self.q["sp"].append(o)
        nc, es = self.nc, self.es
        for eng in ENGS:
            cnt = 0
            sems = []
            for o in self.q[eng]:
                if o.dma or o.fn is None or not o.signal:
                    continue
                ep = cnt // EPOCH
                if ep >= len(sems):
                    sems.append(es.enter_context(nc.semaphore(f"s_{eng}_{ep}")))
                o.sem = sems[ep]
                o.val = cnt % EPOCH + 1
                cnt += 1
        block = es.enter_context(nc.Block())

        def emit(eng, e):
            waited = {}
            for o in self.q[eng]:
                for d in o.deps:
                    k = id(d.sem)
                    if waited.get(k, 0) >= d.val:
                        continue
                    e.wait_ge(d.sem, d.val)
                    waited[k] = d.val
                if o.fn is None:
                    continue
                ins = o.fn(e)
                if o.dma:
                    ins.then_inc(o.sem, 16)
                elif o.signal:
                    ins.then_inc(o.sem, 1)

        @block.tensor
        def _(e):
            emit("pe", e)

        @block.scalar
        def _(e):
            emit("act", e)

        @block.vector
        def _(e):
            emit("dve", e)

        @block.gpsimd
        def _(e):
            emit("pool", e)

        @block.sync
        def _(e):
            emit("sp", e)


class _Stop(Exception):
    pass


def build_program(stage=None):
    nc = bass.Bass("TRN2", target_bir_lowering=False)
    ext = lambda n, s, d=F32: nc.dram_tensor(n, s, d, kind="ExternalInput").ap()
    x_d = ext("x", [T, D])
    ctx_d = ext("ctx", [CTX, D])
    c2_d = ext("c2fm", [128, 32])
    wada_d = ext("wada", [DEPTH, NSLOT_ADA, 128, 4096])
    win_d = ext("win", [DEPTH, NSLOT_IN, 128, 4096])
    wgate_d = ext("wgate", [DEPTH, 128, 256])
    wout_d = ext("wout", [DEPTH, 128, 16 * D])
    pfm_d = ext("pfm", [DEPTH, 128, 112])
    prow_d = ext("prow", [DEPTH, 400])
    mnorm_d = ext("mnorm", [DEPTH, 4, 256])
    cst_d = ext("cst", [128, 1024])
    rope_d = ext("rope", [2, 128, T])
    out_d = nc.dram_tensor("out", [T, D], F32, kind="ExternalOutput").ap()
    yscr_d = nc.dram_tensor("yscr", [NT, D], BF16).ap()
    x1_d = nc.dram_tensor("x1scr", [T, D], F32).ap()
    woutb_d = nc.dram_tensor("woutb", [DEPTH, 128, 16 * D], BF16).ap()
    woutb_k = [Tk() for _ in range(DEPTH)]
    yscr_k = Tk()
    x1_k = Tk()
    dbg_d = nc.dram_tensor("dbg", [128, 8192], F32, kind="ExternalOutput").ap() if stage is not None else None

    with ExitStack() as es:
        P = Prog(nc, es)
        sb = lambda n, s, d: Buf(es.enter_context(nc.sbuf_tensor("sb_" + n, s, d)))
        def ps(n, s, d):
            b = Buf(es.enter_context(nc.psum_tensor("ps_" + n, s, d)))
            b.k.ps = True
            return b

        hT = es.enter_context(nc.sbuf_tensor("sb_hT", [128, 16, NT], BF16))
        hT_k = [Tk() for _ in range(NTILE)]
        R = [sb(f"R{i}", [128, D], F32) for i in range(2)]
        W = [sb(f"W{i}", [128, 4096], BF16) for i in range(3)]
        G = [sb(f"G{i}", [128, 4640], BF16) for i in range(8)]
        GN = R[1]
        cf = sb("cf", [128, 512], F32)
        cb = sb("cb", [128, 1024], BF16)
        pfm = sb("pfm", [128, DEPTH, 112], F32)
        prow = sb("prow", [128, DEPTH, 400], F32)
        mnorm = sb("mnorm", [128, 256], F32)
        wg = sb("wg", [128, 256], BF16)
        sc2 = sb("sc2", [128, 32], BF16)
        c2f = sb("c2f", [128, 32], F32)
        modt = sb("modt", [128, DEPTH, 96], F32)
        Amod = sb("Amod", [128, DEPTH, 2, 16], F32)
        Shm = sb("Shm", [128, DEPTH, 2, 16], F32)
        Gtm = sb("Gtm", [128, DEPTH, 2, 16], F32)
        st = [sb(f"st{i}", [128, 16], F32) for i in range(4)]
        lamt = sb("lamt", [128, 8], F32)
        abias = sb("abias", [128, 4], F32)
        qkst = sb("qkst", [128, 16], F32)
        subw = sb("subw", [128, 128], F32)
        tmpA = [sb(f"tmpA{i}", [128, 512], F32) for i in range(2)]
        tmpB = [sb(f"tmpB{i}", [128, 512], F32) for i in range(2)]
        tbf = [sb(f"tbf{i}", [128, 512], BF16) for i in range(2)]

        carve_state = [5, 0]

        class View:
            def __init__(self, ap):
                self.ap = ap
                self.k = Tk()

            def __getitem__(self, i):
                return self.ap[i]

        def carve(shape, dt):
            n = 1
            for d_ in shape[1:]:
                n *= d_
            nb = n * (4 if dt == F32 else 2)
            nb = (nb + 7) // 8 * 8
            nel = nb // 2
            if carve_state[1] + nel > 4640:
                carve_state[0] += 1
                carve_state[1] = 0
            g, off = carve_state
            assert g <= 7, (shape, carve_state)
            carve_state[1] += nel
            ap = G[g][:, off:off + nel]
            if dt == F32:
                ap = ap.bitcast(F32)
            ap = ap[:, 0:n]
            if len(shape) == 3:
                ap = ap.rearrange("p (a b) -> p a b", b=shape[2])
            return View(ap)

        osb = [carve([128, 4, 128], F32) for i in range(1)]
        ybuf = [carve([128, 4, 256], BF16) for i in range(2)]
        gz = carve([128, NTILE, 16], F32)
        glf = carve([128, NTILE, 8], F32)
        gig = carve([128, NTILE, 8], F32)
        gcum = gz
        ga1 = carve([128, NTILE, 8], F32)
        gwk = carve([128, NTILE, 8], F32)
        gdec = carve([128, NTILE, 8], F32)
        geb = carve([128, NTILE, 8], F32)
        gw1 = carve([128, NTILE, 8], F32)
        SWT = [carve([128, 128], BF16) for i in range(4)]
        kwb = [carve([128, 128], BF16) for i in range(4)]
        numb = [carve([128, 257], F32) for i in range(4)]
        hsum = [carve([128, 256], F32) for i in range(2)]
        Cst = [carve([128, 257], F32) for i in range(2)]
        RING = 4
        Cring = [[carve([128, 257], BF16) for i in range(RING)] for dr_ in range(2)]
        class VW:
            def __init__(self, ap, k=None, psum=False):
                self.ap = ap
                self.k = k if k is not None else Tk()
                if psum:
                    self.k.ps = True

            def __getitem__(self, i):
                return self.ap[i]

        SCA = es.enter_context(nc.psum_tensor("ps_SCA", [128, 1024], F32))
        SCB = es.enter_context(nc.psum_tensor("ps_SCB", [128, 1024], F32))
        FBm = [ps(f"FB{i}", [128, 512], F32) for i in range(2, 6)]
        FB = [VW(SCA[:, 0:512], psum=True), VW(SCA[:, 512:1024], psum=True)] + FBm + \
             [VW(SCB[:, 0:512], psum=True), VW(SCB[:, 512:1024], psum=True)]
        HB = [VW(SCB[:, 0:512].bitcast(BF16), k=FB[6].k), VW(SCB[:, 512:1024].bitcast(BF16), k=FB[7].k)]
        SC2 = [SCA, SCB]
        SC2_k = [[FB[0].k, FB[1].k], [FB[6].k, FB[7].k]]

        ident_f = cf[:, 0:128]
        tri = [cf[:, 128:256], cf[:, 256:384]]
        ones_f = cf[:, 384:512]
        ident_b = cb[:, 0:128]
        nmk = [cb[:, 512:640], cb[:, 640:768]]
        bo_b = cb[:, 768:896]
        perm_b = cb[:, 896:1024]

        qT = G[0][:, 0:NT]
        kT = G[0][:, NT:2 * NT]
        qk_k = [[Tk() for _ in range(5)] for _ in range(2)]
        G0_all = [qk_k[p_][c_] for p_ in range(2) for c_ in range(5)]
        VaA = G[1][:, 0:NTILE * 129].rearrange("p (t c) -> p t c", c=129)
        sgA = G[1][:, 2322:2322 + NT].rearrange("p (t c) -> p t c", c=128)
        PTb = [G[2][:, i * 1024:(i + 1) * 1024] for i in range(4)]
        PT_k = [Tk() for _ in range(4)]
        xs = G[3][:, :].bitcast(F32)
        ys = G[4][:, :].bitcast(F32)
        VaM = G[1][:, 0:NTILE * 257].rearrange("p (t c) -> p t c", c=257)
        ogM = G[2][:, 0:NTILE * 256].rearrange("p (t c) -> p t c", c=256)
        hpart = G[3][:, 0:NTILE * 256].rearrange("p (t c) -> p t c", c=256)

        dma_in = lambda out, in_, w, eng="sp", r=(): P.dma(eng, out, in_, reads=r, writes=w)

        def ck(k, dumps=()):
            if stage == k:
                P.barrier()
                col = 0
                for (ap, n) in dumps:
                    P.dma("pool", dbg_d[0:ap.shape[0], col:col + n], ap)
                    col += n
                raise _Stop()

        def _body():

            dma_in(cf[:, :], cst_d[:, 0:512], [cf])
            dma_in(cb[:, :], cst_d, [cb], eng="pool")
            dma_in(c2f[:, :], c2_d, [c2f])
            for l in range(DEPTH):
                dma_in(pfm[:, l, :], pfm_d[l], [pfm])
                dma_in(prow[:, l, :], prow_d[l].partition_broadcast(128), [prow])
            P.op("act", lambda e: e.activation(out=sc2[:, :], in_=c2f[:, :], func=AF.Silu), [c2f], [sc2])

            wst = {"ctr": 0, "src": [], "issued": 0, "used": 0, "bufs": {}}

            def ws_begin(sources):
                assert wst["used"] == len(wst["src"]), (wst["used"], len(wst["src"]))
                wst["src"] = list(sources)
                wst["issued"] = 0
                wst["used"] = 0
                wst["bufs"] = {}

            def ws_next():
                while wst["issued"] < min(len(wst["src"]), wst["used"] + 3):
                    i = wst["ctr"] % 3
                    wst["ctr"] += 1
                    src = wst["src"][wst["issued"]]
                    P.dma("pool", W[i][:, 0:4096], src[:, 0:4096], writes=[W[i]])
                    wst["bufs"][wst["issued"]] = W[i]
                    wst["issued"] += 1
                b = wst["bufs"][wst["used"]]
                wst["used"] += 1
                return b

            def ada_slot(l, s_):
                wsl = ws_next()
                psm = FB[3]
                w3 = wsl[:, :].rearrange("p (c n) -> p c n", n=256)
                for g in range(2):
                    for c in range(16):
                        P.op("pe", lambda e, w3=w3, g=g, c=c, psm=psm: e.matmul(
                            psm[:, g * 2:g * 2 + 2], lhsT=w3[:, c, g * 128:(g + 1) * 128],
                            rhs=sc2[:, c * 2:c * 2 + 2], start=(c == 0 and g == 0), stop=(c == 15),
                            skip_group_check=True), [wsl, sc2], [psm])
                P.op("dve", lambda e, psm=psm, l=l, s_=s_: e.tensor_tensor(
                    out=modt[:, l, 4 * s_:4 * s_ + 4].rearrange("p (g j) -> p g j", j=2),
                    in0=psm[:, 0:4].rearrange("p (g j) -> p g j", j=2),
                    in1=pfm[:, l, 2 * s_:2 * s_ + 2].unsqueeze(2).to_broadcast([128, 2, 2]), op=ALU.add),
                    [psm, pfm], [modt])

            def ada_finish(l, part):
                m3 = modt[:, l, :].rearrange("p (g j) -> p g j", j=2)
                for j in range(2):
                    if part in (0, 2):
                        P.op("dve", lambda e, m3=m3, j=j, l=l: e.scalar_tensor_tensor(
                            out=Amod[:, l, j, :], in0=m3[:, 16:32, j], scalar=1.0, in1=pfm[:, l, 48:64],
                            op0=ALU.add, op1=ALU.mult), [modt, pfm], [Amod])
                        P.op("dve", lambda e, m3=m3, j=j, l=l: e.tensor_copy(out=Shm[:, l, j, :], in_=m3[:, 0:16, j]),
                             [modt], [Shm])
                    if part in (1, 2):
                        P.op("dve", lambda e, m3=m3, j=j, l=l: e.tensor_tensor(
                            out=Gtm[:, l, j, :], in0=m3[:, 32:48, j], in1=pfm[:, l, 64:80], op=ALU.mult), [modt, pfm], [Gtm])

            ws_begin([wada_d[0, s_] for s_ in range(16)])
            for s_ in range(16):
                ada_slot(0, s_)
            ada_finish(0, 0)

            ck(1, [(modt[:, 0, :], 96), (modt[:, 1, :], 96), (Amod[:, 0, 0, :], 16), (Shm[:, 0, 0, :], 16), (Gtm[:, 0, 0, :], 16)])
            cnt = {"st": 0, "ta": 0, "tb": 0, "tbf": 0, "fb": 0, "hb": 0, "os": 0, "yb": 0, "ev": 0}

            def rot(name, lst):
                i = cnt[name] % len(lst)
                cnt[name] += 1
                return lst[i]

            def rstd_from_ss(ssap, sstile, n, width, out_tile):
                P.op("act", lambda e: e.activation(out=out_tile[:, 8:8 + width], in_=ssap, func=AF.Ln,
                                                   scale=1.0 / n, bias=EPS), [sstile], [out_tile])
                P.op("act", lambda e: e.activation(out=out_tile[:, 0:width], in_=out_tile[:, 8:8 + width],
                                                   func=AF.Exp, scale=-0.5), [out_tile], [out_tile])

            def make_hT(l, tau, xt):
                var = 1 if tau < 2 else 0
                s_ = rot("st", st)
                xn = W[2]
                P.op("act", lambda e: e.activation(out=xn[:, 0:D], in_=xt[:, :], func=AF.Square,
                                                   accum_out=s_[:, 4:5]), [xt], [xn, s_])
                rstd_from_ss(s_[:, 4:5], s_, D, 1, s_)
                P.op("dve", lambda e: e.tensor_scalar(out=xn[:, 0:D], in0=xt[:, :], scalar1=s_[:, 0:1], scalar2=None,
                                                      op0=ALU.mult), [xt, s_], [xn])
                for half in range(2):
                    hb = rot("hb", HB)
                    for c8 in range(8):
                        c = half * 8 + c8
                        P.op("pe", lambda e, hb=hb, c=c, c8=c8: e.transpose(
                            hb[:, c8 * 128:(c8 + 1) * 128], xn[:, c * 128:(c + 1) * 128], ident_b), [xn, cb], [hb])
                    for c8 in range(8):
                        c = half * 8 + c8
                        if half == 0:
                            P.op("act", lambda e, hb=hb, c=c, c8=c8: e.activation(
                                out=hT[:, c, tau * 128:(tau + 1) * 128], in_=hb[:, c8 * 128:(c8 + 1) * 128],
                                func=AF.Identity, scale=Amod[:, l, var, c:c + 1], bias=Shm[:, l, var, c:c + 1]),
                                [hb, Amod, Shm], [hT_k[tau]])
                        else:
                            P.op("dve", lambda e, hb=hb, c=c, c8=c8: e.tensor_scalar(
                                out=hT[:, c, tau * 128:(tau + 1) * 128], in0=hb[:, c8 * 128:(c8 + 1) * 128],
                                scalar1=Amod[:, l, var, c:c + 1], scalar2=Shm[:, l, var, c:c + 1],
                                op0=ALU.mult, op1=ALU.add), [hb, Amod, Shm], [hT_k[tau]])

            def x_src(tau):
                return ctx_d[tau * 128:(tau + 1) * 128, :] if tau < 2 else x_d[(tau - 2) * 128:(tau - 1) * 128, :]

            for tau in range(NTILE):
                xt = R[0]
                dma_in(xt[:, :], x_src(tau), [xt])
                make_hT(0, tau, xt)

            ck(2, [(hT[:, 0, 0:512], 512), (hT[:, 15, 1792:2304], 512)])
            CHUNKS = [(0, 256, [0, 1])] + [(256 + 512 * i, 512, [2 + 4 * i + j for j in range(4)]) for i in range(4)]

            def proj_fm(wsl, coff, c0, n):
                fb = FB[cnt["fb"] % 2]
                cnt["fb"] += 1
                w3 = wsl[:, :].rearrange("p (c n) -> p c n", n=256)
                tiles = range(c0 // 128, (c0 + n) // 128)
                for c in range(16):
                    P.op("pe", lambda e, fb=fb, w3=w3, c=c: e.matmul(
                        fb[:, 0:n], lhsT=w3[:, c, coff:coff + 128], rhs=hT[:, c, c0:c0 + n],
                        start=(c == 0), stop=(c == 15)), [wsl] + [hT_k[t] for t in tiles], [fb])
                return fb

            def proj_tm(wsl, tau, fb, off):
                w3 = wsl[:, :].rearrange("p (c n) -> p c n", n=256)
                for c in range(16):
                    P.op("pe", lambda e, w3=w3, c=c: e.matmul(
                        fb[:, off:off + 256], lhsT=hT[:, c, tau * 128:(tau + 1) * 128], rhs=w3[:, c, :],
                        start=(c == 0), stop=(c == 15)), [wsl, hT_k[tau]], [fb])

            def store_y(l, src_ap, src_buf, tau, col0, ncol):
                P.dma("sp", yscr_d[tau * 128:(tau + 1) * 128, col0:col0 + ncol], src_ap, reads=[src_buf], writes=[yscr_k])

            for l in range(DEPTH):
                last = (l == DEPTH - 1)
                lam_init = 0.8 - 0.6 * math.exp(-0.3 * l)
                seg = []
                for h_ in range(8):
                    if l == 0:
                        seg += [win_d[l, 2 * h_], wada_d[1, 3 * h_], wada_d[0, 16 + h_], win_d[l, 2 * h_ + 1],
                                wada_d[1, 3 * h_ + 1], wada_d[1, 3 * h_ + 2]]
                    else:
                        seg += [win_d[l, 2 * h_], win_d[l, 2 * h_ + 1]]
                for hd_ in range(4):
                    seg += [win_d[l, 16 + 4 * hd_ + j_] for j_ in range(4)]
                ws_begin(seg)
                dma_in(R[0][:, :], rope_d[0], [R[0]])
                dma_in(R[1][:, :], rope_d[1], [R[1]])
                P.dma("pool", wg[:, :], wgate_d[l], writes=[wg])
                pl = prow[:, l, :]
                P.op("dve", lambda e, pl=pl: e.tensor_tensor(out=tmpA[0][:, 0:64], in0=pl[:, 144:208], in1=pl[:, 208:272],
                                                            op=ALU.mult), [prow], [tmpA[0]])
                P.op("dve", lambda e, pl=pl: e.tensor_tensor(out=tmpA[0][:, 64:128], in0=pl[:, 272:336], in1=pl[:, 336:400],
                                                            op=ALU.mult), [prow], [tmpA[0]])
                P.op("dve", lambda e: e.reduce_sum(out=lamt[:, 2:3], in_=tmpA[0][:, 0:64], axis=AX.X), [tmpA[0]], [lamt])
                P.op("dve", lambda e: e.reduce_sum(out=lamt[:, 3:4], in_=tmpA[0][:, 64:128], axis=AX.X), [tmpA[0]], [lamt])
                P.op("act", lambda e: e.activation(out=lamt[:, 4:6], in_=lamt[:, 2:4], func=AF.Exp), [lamt], [lamt])
                P.op("dve", lambda e, li=lam_init: e.scalar_tensor_tensor(
                    out=lamt[:, 0:1], in0=lamt[:, 4:5], scalar=li, in1=lamt[:, 5:6], op0=ALU.add, op1=ALU.subtract),
                    [lamt], [lamt])
                P.op("dve", lambda e: e.tensor_scalar(out=lamt[:, 1:2], in0=lamt[:, 0:1], scalar1=-1.0, scalar2=None,
                                                      op0=ALU.mult), [lamt], [lamt])
                P.op("dve", lambda e, pl=pl, li=lam_init: e.tensor_scalar(
                    out=subw[:, :], in0=pl[:, 0:128], scalar1=1.0 - li, scalar2=None, op0=ALU.mult), [prow], [subw])

                for h in range(8):
                    wqk = ws_next()
                    P.op("pool", lambda e: e.memset(VaA[:, :, 128:129], 1.0), [], [G[1]])
                    items = [(ci_, c0, n, part) for ci_, (c0, n, tiles) in enumerate(CHUNKS) for part in range(2)]
                    pbanks = [FB[0], FB[1], FB[6]]
                    stA = {}

                    def stage_A(i):
                        ci_, c0, n, part = items[i]
                        dst = qT if part == 0 else kT
                        fb = pbanks[i % 3]
                        w3 = wqk[:, :].rearrange("p (c n) -> p c n", n=256)
                        tls = range(c0 // 128, (c0 + n) // 128)
                        for c in range(16):
                            P.op("pe", lambda e, fb=fb, w3=w3, c=c, c0=c0, n=n, part=part: e.matmul(
                                fb[:, 0:n], lhsT=w3[:, c, part * 128:(part + 1) * 128], rhs=hT[:, c, c0:c0 + n],
                                start=(c == 0), stop=(c == 15)), [wqk] + [hT_k[t] for t in tls], [fb])
                        if c0 < CTX:
                            P.op("act", lambda e, fb=fb, dst=dst, c0=c0, n=n: e.activation(
                                out=dst[:, c0:c0 + n], in_=fb[:, 0:n], func=AF.Identity), [fb], [qk_k[part][ci_]])
                        else:
                            tb_i = 2 + (i % 2)
                            P.op("act", lambda e, fb=fb, tb_i=tb_i, n=n: e.activation(
                                out=PTb[tb_i][:, 0:n], in_=fb[:, 0:n], func=AF.Identity), [fb], [PT_k[tb_i]])
                            stA[i] = (fb, tb_i)

                    def stage_B(i):
                        ci_, c0, n, part = items[i]
                        if c0 < CTX:
                            return
                        dst = qT if part == 0 else kT
                        fb, tb_i = stA[i]
                        t0 = c0 - CTX
                        f2 = FB[2] if i % 2 == 0 else FB[7]
                        ta_ = rot("ta", tmpA)
                        tb2 = rot("tb", tmpB)
                        P.op("pe", lambda e, tb_i=tb_i, f2=f2, n=n: e.matmul(
                            f2[:, 0:n], lhsT=perm_b, rhs=PTb[tb_i][:, 0:n], start=True, stop=True), [PT_k[tb_i], cb], [f2])
                        P.op("dve", lambda e, fb=fb, ta_=ta_, t0=t0, n=n: e.tensor_tensor(
                            out=ta_[:, 0:n], in0=fb[:, 0:n], in1=R[0][:, t0:t0 + n], op=ALU.mult), [fb, R[0]], [ta_])
                        P.op("dve", lambda e, f2=f2, tb2=tb2, t0=t0, n=n: e.tensor_tensor(
                            out=tb2[:, 0:n], in0=f2[:, 0:n], in1=R[1][:, t0:t0 + n], op=ALU.mult), [f2, R[1]], [tb2])
                        P.op("dve", lambda e, ta_=ta_, tb2=tb2, dst=dst, c0=c0, n=n: e.tensor_tensor(
                            out=dst[:, c0:c0 + n], in0=ta_[:, 0:n], in1=tb2[:, 0:n], op=ALU.add), [ta_, tb2], [qk_k[part][ci_]])

                    def stage_C1(i):
                        ci_, c0, n, part = items[i]
                        dst = qT if part == 0 else kT
                        tq_i = i % 2
                        P.op("act", lambda e, dst=dst, tq_i=tq_i, c0=c0, n=n: e.activation(
                            out=PTb[tq_i][:, 0:n], in_=dst[:, c0:c0 + n], func=AF.Square), [qk_k[part][ci_]], [PT_k[tq_i]])

                    def stage_C(i):
                        ci_, c0, n, part = items[i]
                        tq_i = i % 2
                        f3 = FB[3]
                        P.op("pe", lambda e, tq_i=tq_i, f3=f3, n=n: e.matmul(
                            f3[:, 0:n], lhsT=bo_b, rhs=PTb[tq_i][:, 0:n], start=True, stop=True), [PT_k[tq_i], cb], [f3])
                        col = part * 5 + ci_
                        P.op("dve", lambda e, f3=f3, col=col, n=n: e.reduce_max(
                            out=qkst[:, col:col + 1], in_=f3[:, 0:n], axis=AX.X), [f3], [qkst])

                    NI = len(items)
                    for i in range(NI + 3):
                        if 0 <= i - 3 < NI:
                            stage_C(i - 3)
                        if 0 <= i - 2 < NI:
                            stage_C1(i - 2)
                        if i < NI:
                            stage_A(i)
                        if 0 <= i - 1 < NI:
                            stage_B(i - 1)
                    P.op("dve", lambda e: e.reduce_max(out=qkst[:, 10:11], in_=qkst[:, 0:5], axis=AX.X), [qkst], [qkst])
                    P.op("dve", lambda e: e.reduce_max(out=qkst[:, 11:12], in_=qkst[:, 5:10], axis=AX.X), [qkst], [qkst])
                    P.op("dve", lambda e: e.tensor_tensor(out=qkst[:, 12:13], in0=qkst[:, 10:11], in1=qkst[:, 11:12],
                                                          op=ALU.mult), [qkst], [qkst])
                    P.op("dve", lambda e: e.tensor_scalar(out=qkst[:, 14:15], in0=bo_b[:, 0:1], scalar1=qkst[:, 12:13],
                                                          scalar2=None, op0=ALU.mult), [qkst, cb], [qkst])
                    P.op("dve", lambda e: e.tensor_scalar(out=qkst[:, 15:16], in0=bo_b[:, 64:65], scalar1=qkst[:, 12:13],
                                                          scalar2=None, op0=ALU.mult), [qkst, cb], [qkst])
                    f3 = FB[3]
                    P.op("pe", lambda e, f3=f3: e.matmul(f3[:, 0:2], lhsT=ones_f, rhs=qkst[:, 14:16],
                                                         start=True, stop=True), [qkst, cf], [f3])
                    P.op("dve", lambda e, f3=f3: e.reduce_max(out=abias[:, 1:2], in_=f3[:, 0:2], axis=AX.X), [f3], [abias])
                    P.op("act", lambda e: e.activation(out=abias[:, 2:3], in_=abias[:, 1:2], func=AF.Ln,
                                                       scale=1.0 / 64.0, bias=1e-30), [abias], [abias])
                    P.op("act", lambda e: e.activation(out=abias[:, 3:4], in_=abias[:, 2:3], func=AF.Exp, scale=0.5),
                         [abias], [abias])
                    P.op("dve", lambda e: e.tensor_scalar(out=abias[:, 0:1], in0=abias[:, 3:4], scalar1=-1.05 / 8.0,
                                                          scalar2=None, op0=ALU.mult), [abias], [abias])
                    if h == 0:
                        ck(3, [(qT[:, 0:512], 512), (kT[:, 0:512], 512), (qT[:, 1792:2304], 512), (kT[:, 1792:2304], 512), (abias[:, 0:4], 4)])
                    if l == 0:
                        ada_slot(1, 3 * h)
                        ada_slot(0, 16 + h)
                    wvg = ws_next()
                    for t2 in range(0, NTILE, 2):
                        fb = FB[4 + (t2 // 2) % 2]
                        for j in range(2):
                            proj_tm(wvg, t2 + j, fb, j * 256)
                        f3v = fb[:, :].rearrange("p (j c) -> p j c", c=256)
                        P.op("dve", lambda e, f3v=f3v, t2=t2: e.tensor_copy(
                            out=VaA[:, t2:t2 + 2, 0:128], in_=f3v[:, :, 0:128]), [fb], [G[1]])
                        P.op("act", lambda e, f3v=f3v, t2=t2: e.activation(
                            out=sgA[:, t2:t2 + 2, :], in_=f3v[:, :, 128:256], func=AF.Silu), [fb], [G[1]])
                    P.op("pool", lambda e: e.tensor_tensor(
                        out=sgA[:, :, :], in0=sgA[:, :, :], in1=subw[:, :].unsqueeze(1).to_broadcast([128, NTILE, 128]),
                        op=ALU.mult), [G[1], subw], [G[1]])

                    if h == 0:
                        ck(4, [(VaA[:, 0, :], 129), (sgA[:, 0, :], 128), (VaA[:, 17, :], 129), (sgA[:, 17, :], 128)])
                    if l == 0:
                        ada_slot(1, 3 * h + 1)
                        ada_slot(1, 3 * h + 2)
                    def emit_S(ch, ki):
                        q0, qn, ktiles, tq_tiles = ch
                        b = ki % 2
                        kt = ktiles[ki]
                        for sub in range(2):
                            fs = FB[0 + sub] if b == 0 else FB[6 + sub]
                            P.op("pe", lambda e, fs=fs, sub=sub, kt=kt, q0=q0, qn=qn: e.matmul(
                                fs[:, 0:qn], lhsT=kT[sub * 64:(sub + 1) * 64, kt * 128:(kt + 1) * 128],
                                rhs=qT[sub * 64:(sub + 1) * 64, q0:q0 + qn], start=True, stop=True),
                                [qk_k[1][0 if kt < 2 else 1 + (kt - 2) // 4], qk_k[0][0 if q0 == 0 else 1 + (q0 - 256) // 512]], [fs])

                    def emit_exp(ch, ki):
                        q0, qn, ktiles, tq_tiles = ch
                        b = ki % 2
                        pt, ptk = PTb[b], PT_k[b]
                        sc3 = SC2[b][:, :].rearrange("p (s c) -> p s c", c=512)
                        pt3 = pt.rearrange("p (s c) -> p s c", c=512)
                        P.op("act", lambda e, sc3=sc3, pt3=pt3, qn=qn: e.activation(
                            out=pt3[:, :, 0:qn], in_=sc3[:, :, 0:qn], func=AF.Exp, scale=0.125, bias=abias[:, 0:1]),
                            [SC2_k[b][0], SC2_k[b][1], abias], [ptk])

                    def emit_PV(ch, ki):
                        q0, qn, ktiles, tq_tiles = ch
                        nj = qn // 128
                        nk = len(ktiles)
                        kt = ktiles[ki]
                        pt, ptk = PTb[ki % 2], PT_k[ki % 2]
                        for sub in range(2):
                            for j in range(nj):
                                P.op("pe", lambda e, pt=pt, j=j, sub=sub, kt=kt, ki=ki, nk=nk: e.matmul(
                                    FB[2 + j][:, sub * 129:(sub + 1) * 129],
                                    lhsT=pt[:, sub * 512 + j * 128:sub * 512 + (j + 1) * 128],
                                    rhs=VaA[:, kt, :], start=(ki == 0 and sub == 0), stop=(ki == nk - 1),
                                    skip_group_check=True), [ptk, G[1]], [FB[2 + j]])

                    fin_state = {}

                    def fin1(ch):
                        q0, qn, ktiles, tq_tiles = ch
                        nj = qn // 128
                        os_ = rot("os", osb)
                        s_ = rot("st", st)
                        a3s = [FB[2 + j][:, 0:258].rearrange("p (s c) -> p s c", c=129) for j in range(nj)]
                        tA = [tmpA[0], tmpA[1]]
                        for j in range(nj):
                            a3 = a3s[j]
                            P.op("dve", lambda e, a3=a3, j=j, s_=s_: e.reciprocal(out=s_[:, 8 + 2 * j:10 + 2 * j], in_=a3[:, :, 128]),
                                 [FB[2 + j]], [s_])
                            P.op("dve", lambda e, j=j, s_=s_: e.tensor_tensor(
                                out=s_[:, 9 + 2 * j:10 + 2 * j], in0=s_[:, 9 + 2 * j:10 + 2 * j], in1=lamt[:, 1:2],
                                op=ALU.mult), [s_, lamt], [s_])
                        for j in range(nj):
                            a3 = a3s[j]
                            ta_ = tA[j // 2]
                            P.op("act", lambda e, a3=a3, j=j, ta_=ta_, s_=s_: e.activation(
                                out=ta_[:, (j % 2) * 128:(j % 2) * 128 + 128], in_=a3[:, 0, 0:128], func=AF.Identity,
                                scale=s_[:, 8 + 2 * j:9 + 2 * j]), [FB[2 + j], s_], [ta_])
                            P.op("dve", lambda e, a3=a3, j=j, ta_=ta_, s_=s_, os_=os_: e.scalar_tensor_tensor(
                                out=os_[:, j, :], in0=a3[:, 1, 0:128], scalar=s_[:, 9 + 2 * j:10 + 2 * j],
                                in1=ta_[:, (j % 2) * 128:(j % 2) * 128 + 128],
                                op0=ALU.mult, op1=ALU.add), [FB[2 + j], s_, ta_], [os_])
                        fin_state["p"] = (ch, os_, s_)

                    def fin2():
                        if "p" not in fin_state:
                            return
                        ch, os_, s_ = fin_state.pop("p")
                        q0, qn, ktiles, tq_tiles = ch
                        nj = qn // 128
                        yb = rot("yb", ybuf)
                        for j in range(nj):
                            tb2 = rot("tb", tmpB)
                            P.op("act", lambda e, j=j, tb2=tb2, os_=os_, s_=s_: e.activation(
                                out=tb2[:, 0:128], in_=os_[:, j, :], func=AF.Square, accum_out=s_[:, 4 + j:5 + j]),
                                [os_], [tb2, s_])
                        s2 = rot("st", st)
                        P.op("act", lambda e, s_=s_, s2=s2, nj=nj: e.activation(out=s2[:, 8:8 + nj], in_=s_[:, 4:4 + nj], func=AF.Ln,
                                                                                 scale=1.0 / 128, bias=EPS), [s_], [s2])
                        P.op("act", lambda e, s2=s2, nj=nj: e.activation(out=s2[:, 0:nj], in_=s2[:, 8:8 + nj], func=AF.Exp,
                                                                         scale=-0.5), [s2], [s2])
                        for j in range(nj):
                            tau = tq_tiles[j]
                            P.op("dve", lambda e, j=j, tau=tau, os_=os_, s2=s2, yb=yb: e.scalar_tensor_tensor(
                                out=yb[:, j, 0:128], in0=os_[:, j, :], scalar=s2[:, j:j + 1], in1=sgA[:, tau, :],
                                op0=ALU.mult, op1=ALU.mult), [os_, s2, G[1]], [yb])
                            store_y(l, yb[:, j, 0:128], yb, tau, h * 128, 128)

                    chunks = []
                    if not last:
                        chunks.append((0, 256, [0, 1], [0, 1]))
                    for qc in range(4):
                        chunks.append((256 + 512 * qc, 512, list(range(NTILE)), [2 + 4 * qc + j for j in range(4)]))
                    for ci_c, ch in enumerate(chunks):
                        nk = len(ch[2])
                        if ci_c == 0:
                            emit_S(ch, 0)
                        for ki in range(nk):
                            if ki + 1 < nk:
                                emit_S(ch, ki + 1)
                            if not (ki == 0 and ci_c > 0):
                                emit_exp(ch, ki)
                            emit_PV(ch, ki)
                            if ki == 1:
                                fin2()
                        if ci_c + 1 < len(chunks):
                            nx = chunks[ci_c + 1]
                            emit_S(nx, 0)
                            emit_exp(nx, 0)
                        fin1(ch)
                    fin2()
                    if l == 0 and h == 7:
                        ada_finish(1, 2)
                        ada_finish(0, 1)

                    if h == 0:
                        ck(5, [(yscr_d[0:128, 0:128], 128), (yscr_d[256:384, 0:128], 128), (yscr_d[2176:2304, 0:128], 128)])
                for i8 in range(8):
                    P.dma("pool", woutb_d[l, :, i8 * 4096:(i8 + 1) * 4096], wout_d[l, :, i8 * 4096:(i8 + 1) * 4096],
                          writes=[woutb_k[l]])
                fg = FB[0]
                for tau in range(NTILE):
                    w3g = wg[:, :].rearrange("p (c n) -> p c n", n=16)
                    for c in range(16):
                        P.op("pe", lambda e, tau=tau, c=c, w3g=w3g: e.matmul(
                            fg[:, tau * 16:(tau + 1) * 16], lhsT=hT[:, c, tau * 128:(tau + 1) * 128], rhs=w3g[:, c, :],
                            start=(c == 0), stop=(c == 15)), [wg, hT_k[tau]], [fg])
                fg3 = fg[:, 0:NTILE * 16].rearrange("p (t c) -> p t c", c=16)
                P.op("dve", lambda e, fg3=fg3, l=l: e.tensor_tensor(
                    out=gz[:, :, :], in0=fg3, in1=prow[:, l, 128:144].unsqueeze(1).to_broadcast([128, NTILE, 16]),
                    op=ALU.add), [fg, prow], [gz])
                P.op("act", lambda e: e.activation(out=gz[:, :, :], in_=gz[:, :, :], func=AF.Tanh, scale=1.0 / 15.0),
                     [gz], [gz])
                P.op("dve", lambda e: e.tensor_scalar(out=gig[:, :, :], in0=gz[:, :, 0:8], scalar1=15.0, scalar2=LN_KS,
                                                      op0=ALU.mult, op1=ALU.add), [gz], [gig])
                P.op("act", lambda e: e.activation(out=glf[:, :, :], in_=gz[:, :, 8:16], func=AF.Exp, scale=-15.0),
                     [gz], [glf])
                P.op("act", lambda e: e.activation(out=glf[:, :, :], in_=glf[:, :, :], func=AF.Ln, bias=1.0),
                     [glf], [glf])
                P.op("dve", lambda e: e.tensor_scalar(out=glf[:, :, :], in0=glf[:, :, :], scalar1=-1.0, scalar2=None,
                                                      op0=ALU.mult), [glf], [glf])
                fc = FB[1]
                for tau in range(NTILE):
                    for dr in range(2):
                        P.op("pe", lambda e, tau=tau, dr=dr: e.matmul(
                            fc[:, tau * 16 + dr * 4:tau * 16 + dr * 4 + 4], lhsT=tri[dr],
                            rhs=glf[:, tau, dr * 4:dr * 4 + 4], start=True, stop=True), [cf, glf], [fc])
                    P.op("pe", lambda e, tau=tau: e.matmul(
                        fc[:, tau * 16 + 8:tau * 16 + 16], lhsT=ones_f, rhs=glf[:, tau, :], start=True, stop=True),
                        [cf, glf], [fc])
                fc3 = fc[:, 0:NTILE * 16].rearrange("p (t c) -> p t c", c=16)
                P.op("dve", lambda e, fc3=fc3: e.tensor_copy(out=gcum[:, :, :], in_=fc3), [fc], [gcum])
                P.op("dve", lambda e: e.tensor_tensor(out=ga1[:, :, :], in0=gig[:, :, :], in1=gcum[:, :, 0:8],
                                                      op=ALU.subtract), [gig, gcum], [ga1])
                P.op("dve", lambda e: e.tensor_tensor(out=gwk[:, :, :], in0=ga1[:, :, :], in1=gcum[:, :, 8:16],
                                                      op=ALU.add), [ga1, gcum], [gwk])
                P.op("act", lambda e: e.activation(out=gwk[:, :, :], in_=gwk[:, :, :], func=AF.Exp), [gwk], [gwk])
                P.op("act", lambda e: e.activation(out=gdec[:, :, :], in_=gcum[:, :, 8:16], func=AF.Exp), [gcum], [gdec])
                P.op("act", lambda e: e.activation(out=geb[:, :, :], in_=gcum[:, :, 0:8], func=AF.Exp), [gcum], [geb])
                P.op("act", lambda e: e.activation(out=gw1[:, :, :], in_=ga1[:, :, :], func=AF.Exp), [ga1], [gw1])

                ck(6, [(glf[:, :, :], 144), (gig[:, :, :], 144), (gcum[:, :, :], 288), (gwk[:, :, :], 144)])
                for hd in range(4):
                    wqk = ws_next()
                    P.dma("sp", mnorm[:, :], mnorm_d[l, hd].partition_broadcast(128), writes=[mnorm])
                    for part in range(2):
                        dst = qT if part == 0 else kT
                        P.op("pool", lambda e: e.memset(xs[:, 0:2320], 0.0), [], [G[3]])
                        for (c0, n, tiles) in CHUNKS:
                            fb = proj_fm(wqk, part * 128, c0, n)
                            so = 1 + c0 if c0 < CTX else 3 + c0
                            P.op("act", lambda e, fb=fb, so=so, n=n: e.activation(
                                out=xs[:, so:so + n], in_=fb[:, 0:n], func=AF.Identity), [fb], [G[3]])
                        ci = part * 4 + hd
                        cw = lambda j, ci=ci, l=l: pfm[:, l, 80 + ci * 3 + j:80 + ci * 3 + j + 1]
                        cbias = pfm[:, l, 104 + ci:105 + ci]
                        P.op("dve", lambda e, cw=cw, cbias=cbias: e.tensor_scalar(
                            out=ys[:, 1:2307], in0=xs[:, 1:2307], scalar1=cw(1), scalar2=cbias, op0=ALU.mult, op1=ALU.add),
                            [G[3], pfm], [G[4]])
                        P.op("dve", lambda e, cw=cw: e.scalar_tensor_tensor(
                            out=ys[:, 1:2307], in0=xs[:, 0:2306], scalar=cw(0), in1=ys[:, 1:2307], op0=ALU.mult, op1=ALU.add),
                            [G[3], G[4], pfm], [G[4]])
                        P.op("dve", lambda e, cw=cw: e.scalar_tensor_tensor(
                            out=ys[:, 1:2307], in0=xs[:, 2:2308], scalar=cw(2), in1=ys[:, 1:2307], op0=ALU.mult, op1=ALU.add),
                            [G[3], G[4], pfm], [G[4]])
                        P.op("act", lambda e, dst=dst: e.activation(out=dst[:, 0:CTX], in_=ys[:, 1:257], func=AF.Silu),
                             [G[4]], G0_all)
                        P.op("act", lambda e, dst=dst: e.activation(out=dst[:, CTX:NT], in_=ys[:, 259:2307], func=AF.Silu),
                             [G[4]], G0_all)
                    P.op("pool", lambda e: e.memset(VaM[:, :, 256:257], 1.0), [], [G[1]])
                    wv = ws_next()
                    for t2 in range(0, NTILE, 2):
                        fb = FB[4 + (t2 // 2) % 2]
                        for j in range(2):
                            proj_tm(wv, t2 + j, fb, j * 256)
                        f3v = fb[:, :].rearrange("p (j c) -> p j c", c=256)
                        P.op("dve", lambda e, f3v=f3v, t2=t2: e.tensor_copy(out=VaM[:, t2:t2 + 2, 0:256], in_=f3v),
                             [fb], [G[1]])
                    wo = ws_next()
                    for t2 in range(0, NTILE, 2):
                        fb = FB[4 + (t2 // 2) % 2]
                        for j in range(2):
                            proj_tm(wo, t2 + j, fb, j * 256)
                        f3v = fb[:, :].rearrange("p (j c) -> p j c", c=256)
                        P.op("act", lambda e, f3v=f3v, t2=t2: e.activation(out=ogM[:, t2:t2 + 2, :], in_=f3v, func=AF.Sigmoid),
                             [fb], [G[2]])
                    wgg = ws_next()
                    for t2 in range(0, NTILE, 2):
                        fb = FB[4 + (t2 // 2) % 2]
                        for j in range(2):
                            proj_tm(wgg, t2 + j, fb, j * 256)
                        f3v = fb[:, :].rearrange("p (j c) -> p j c", c=256)
                        tb_ = rot("tbf", tbf)
                        tb3 = tb_[:, :].rearrange("p (j c) -> p j c", c=256)
                        P.op("act", lambda e, f3v=f3v, tb3=tb3: e.activation(out=tb3, in_=f3v, func=AF.Silu), [fb], [tb_])
                        P.op("pool", lambda e, tb3=tb3, t2=t2: e.tensor_tensor(
                            out=ogM[:, t2:t2 + 2, :], in0=ogM[:, t2:t2 + 2, :], in1=tb3, op=ALU.mult), [tb_, G[2]], [G[2]])

                    orders = [list(range(NTILE)), [1, 0] + list(range(NTILE - 1, 1, -1))]
                    pos = [{t: i for i, t in enumerate(o)} for o in orders]
                    mask01 = [cb[:, 128:256], cb[:, 256:384]]
                    LEAD = 2

                    def stage_state(dr, s_i):
                        tau = orders[dr][s_i]
                        gi = dr * 4 + hd
                        tsl = slice(tau * 128, (tau + 1) * 128)
                        hbk = HB[dr]
                        bU = FB[6 + dr]
                        yield P.op("pe", lambda e, hbk=hbk, tsl=tsl: e.transpose(hbk[:, 640:768], kT[:, tsl], ident_b),
                             G0_all + [cb], [hbk])
                        kw_ = kwb[dr * 2 + s_i % 2]
                        yield P.op("act", lambda e, hbk=hbk, kw_=kw_, tau=tau, gi=gi: e.activation(
                            out=kw_[:, :], in_=hbk[:, 640:768], func=AF.Identity, scale=gwk[:, tau, gi:gi + 1]),
                            [hbk, gwk], [kw_])
                        yield P.op("pe", lambda e, bU=bU, kw_=kw_, tau=tau: e.matmul(
                            bU[:, 0:257], lhsT=kw_[:, :], rhs=VaM[:, tau, :], start=True, stop=True), [kw_, G[1]], [bU])
                        if s_i == 0:
                            yield P.op("dve", lambda e, bU=bU, dr=dr: e.tensor_copy(out=Cst[dr][:, :], in_=bU[:, 0:257]),
                                 [bU], [Cst[dr]])
                        else:
                            yield P.op("dve", lambda e, bU=bU, dr=dr, tau=tau, gi=gi: e.scalar_tensor_tensor(
                                out=Cst[dr][:, :], in0=Cst[dr][:, :], scalar=gdec[:, tau, gi:gi + 1], in1=bU[:, 0:257],
                                op0=ALU.mult, op1=ALU.add), [Cst[dr], gdec, bU], [Cst[dr]])
                        slot = Cring[dr][s_i % RING]
                        yield P.op("act", lambda e, dr=dr, slot=slot: e.activation(out=slot[:, :], in_=Cst[dr][:, :],
                                                                             func=AF.Identity), [Cst[dr]], [slot])

                    def stage_out(dr, s_i):
                        tau = orders[dr][s_i]
                        if last and tau < 2:
                            return
                        gi = dr * 4 + hd
                        tsl = slice(tau * 128, (tau + 1) * 128)
                        bS = FB[dr]
                        yield P.op("pe", lambda e, bS=bS, tsl=tsl: e.matmul(
                            bS[:, 0:128], lhsT=kT[:, tsl], rhs=qT[:, tsl], start=True, stop=True), G0_all, [bS])
                        sw_ = SWT[dr * 2 + s_i % 2]
                        yield P.op("dve", lambda e, bS=bS, sw_=sw_, tau=tau, gi=gi, dr=dr: e.scalar_tensor_tensor(
                            out=sw_[:, :], in0=bS[:, 0:128], scalar=gw1[:, tau, gi:gi + 1], in1=mask01[dr],
                            op0=ALU.mult, op1=ALU.mult), [bS, gw1, cb], [sw_])
                        bH = FB[2 + 2 * dr + s_i % 2]
                        yield P.op("pe", lambda e, bH=bH, sw_=sw_, tau=tau, s_i=s_i: e.matmul(
                            bH[:, 0:257], lhsT=sw_[:, :], rhs=VaM[:, tau, :], start=True, stop=(s_i == 0)),
                            [sw_, G[1]], [bH])
                        if s_i > 0:
                            slot = Cring[dr][(s_i - 1) % RING]
                            yield P.op("pe", lambda e, bH=bH, tsl=tsl, slot=slot: e.matmul(
                                bH[:, 0:257], lhsT=qT[:, tsl], rhs=slot[:, :], start=False, stop=True),
                                G0_all + [slot], [bH])
                        nm_ = numb[dr * 2 + s_i % 2]
                        yield P.op("act", lambda e, bH=bH, nm_=nm_, tau=tau, gi=gi: e.activation(
                            out=nm_[:, :], in_=bH[:, 0:257], func=AF.Identity, scale=geb[:, tau, gi:gi + 1]),
                            [bH, geb], [nm_])
                        s_ = rot("st", st)
                        yield P.op("act", lambda e, nm_=nm_, s_=s_: e.activation(out=s_[:, 0:1], in_=nm_[:, 256:257], func=AF.Abs),
                             [nm_], [s_])
                        yield P.op("dve", lambda e, s_=s_: e.tensor_scalar(
                            out=s_[:, 1:2], in0=s_[:, 0:1], scalar1=1.0, scalar2=None, op0=ALU.max), [s_], [s_])
                        yield P.op("dve", lambda e, s_=s_: e.reciprocal(out=s_[:, 2:3], in_=s_[:, 1:2]), [s_], [s_])
                        second = pos[dr][tau] > pos[1 - dr][tau]
                        if not second:
                            yield P.op("act", lambda e, nm_=nm_, s_=s_, tau=tau: e.activation(
                                out=hpart[:, tau, :], in_=nm_[:, 0:256], func=AF.Identity, scale=s_[:, 2:3]),
                                [nm_, s_], [G[3]])
                        else:
                            hm = hsum[dr]
                            yield P.op("dve", lambda e, nm_=nm_, s_=s_, tau=tau, hm=hm: e.scalar_tensor_tensor(
                                out=hm[:, :], in0=nm_[:, 0:256], scalar=s_[:, 2:3], in1=hpart[:, tau, :],
                                op0=ALU.mult, op1=ALU.add), [nm_, s_, G[3]], [hm])
                            tb2 = rot("tb", tmpB)
                            s3 = rot("st", st)
                            yield P.op("act", lambda e, hm=hm, tb2=tb2, s3=s3: e.activation(
                                out=tb2[:, 0:256], in_=hm[:, :], func=AF.Square, accum_out=s3[:, 4:5]),
                                [hm], [tb2, s3])
                            rstd_from_ss(s3[:, 4:5], s3, 256, 1, s3)
                            yield P.op("dve", lambda e, hm=hm, s3=s3: e.scalar_tensor_tensor(
                                out=hm[:, :], in0=hm[:, :], scalar=s3[:, 0:1], in1=mnorm[:, :], op0=ALU.mult,
                                op1=ALU.mult), [hm, s3, mnorm], [hm])
                            yb = rot("yb", ybuf)
                            yield P.op("dve", lambda e, hm=hm, yb=yb, tau=tau: e.tensor_tensor(
                                out=yb[:, 0, :], in0=hm[:, :], in1=ogM[:, tau, :], op=ALU.mult), [hm, G[2]], [yb])
                            store_y(l, yb[:, 0, :], yb, tau, 1024 + hd * 256, 256)

                    def run_rr(gens):
                        gens = list(gens)
                        while gens:
                            nxt = []
                            for g_ in gens:
                                try:
                                    next(g_)
                                    nxt.append(g_)
                                except StopIteration:
                                    pass
                            gens = nxt

                    for it in range(-LEAD, NTILE):
                        gens = []
                        for dr in range(2):
                            sa = it + LEAD
                            if 0 <= sa <= NTILE - 2:
                                gens.append(stage_state(dr, sa))
                        for dr in range(2):
                            if it >= 0:
                                gens.append(stage_out(dr, it))
                        run_rr(gens)
                ck(8, [(yscr_d[0:128, 1024:1536], 512), (yscr_d[256:384, 1024:1536], 512), (yscr_d[2176:2304, 1024:1536], 512), (yscr_d[2176:2304, 0:1024], 1024)])
                P.barrier()
                for i in range(8):
                    P.dma("sp", G[i][:, 0:4096], woutb_d[l, :, i * 4096:(i + 1) * 4096], reads=[woutb_k[l]], writes=[G[i]])
                tiles = list(range(NTILE)) if not last else list(range(2, NTILE))
                xt = R[0]
                yt = W[0]
                yT_ = W[1]
                YB = [VW(FB[4][:, :].bitcast(BF16), k=FB[4].k), VW(FB[5][:, :].bitcast(BF16), k=FB[5].k)]
                Ttmp = [tmpA[0], tmpA[1], tmpB[0], tmpB[1]]
                xn = W[2]

                def gn_build(var):
                    for c in range(16):
                        ta_ = rot("ta", tmpA)
                        P.op("pool", lambda e, ta_=ta_, c=c, var=var, l=l: e.tensor_scalar(
                            out=ta_[:, 0:128], in0=ones_f, scalar1=Gtm[:, l, var, c:c + 1], scalar2=None, op0=ALU.mult),
                            [cf, Gtm], [ta_])
                        fq = HB[c % 2]
                        P.op("pe", lambda e, ta_=ta_, fq=fq: e.matmul(FB[6 + (0 if fq is HB[0] else 1)][:, 0:128],
                                                                      lhsT=ta_[:, 0:128], rhs=ident_f,
                                                                      start=True, stop=True), [ta_, cf], [fq])
                        P.op("act", lambda e, fq=fq, c=c: e.activation(
                            out=GN[:, c * 128:(c + 1) * 128], in_=FB[6 + (0 if fq is HB[0] else 1)][:, 0:128],
                            func=AF.Identity), [fq], [GN])

                yt_k = [Tk(), Tk()]
                yT_k = [Tk(), Tk()]
                P.op("pool", lambda e: e.memset(yt[:, 0:8], 0.0), [], [yt, yt_k[0], yt_k[1]])
                P.op("pool", lambda e: e.memset(yT_[:, 0:8], 0.0), [], [yT_, yT_k[0], yT_k[1]])

                def YLOAD(tau):
                    yoff = (tau % 2) * 2048
                    P.dma("sp", yt[:, yoff:yoff + 2048], yscr_d[tau * 128:(tau + 1) * 128, :], reads=[yscr_k],
                          writes=[yt_k[tau % 2]])

                def E1a(tau):
                    yoff = (tau % 2) * 2048
                    toff = yoff
                    for half in range(2):
                        hb = YB[half]
                        for c8 in range(8):
                            c = half * 8 + c8
                            P.op("pe", lambda e, hb=hb, c=c, c8=c8, yoff=yoff: e.transpose(
                                hb[:, c8 * 128:(c8 + 1) * 128], yt[:, yoff + c * 128:yoff + (c + 1) * 128], ident_b),
                                [yt_k[tau % 2], cb], [hb])
                        if half == 0:
                            P.op("act", lambda e, hb=hb, half=half, toff=toff: e.activation(
                                out=yT_[:, toff + half * 1024:toff + (half + 1) * 1024], in_=hb[:, :], func=AF.Identity),
                                [hb], [yT_k[tau % 2]])
                        else:
                            P.op("dve", lambda e, hb=hb, half=half, toff=toff: e.tensor_copy(
                                out=yT_[:, toff + half * 1024:toff + (half + 1) * 1024], in_=hb[:, :]), [hb], [yT_k[tau % 2]])
                def E1b(tau):
                    toff = (tau % 2) * 2048
                    for cg in range(4):
                        fo = FB[cg]
                        for c in range(16):
                            P.op("pe", lambda e, fo=fo, c=c, cg=cg, toff=toff: e.matmul(
                                fo[:, :], lhsT=yT_[:, toff + c * 128:toff + (c + 1) * 128],
                                rhs=G[c // 2][:, (c % 2) * 2048 + cg * 512:(c % 2) * 2048 + (cg + 1) * 512],
                                start=(c == 0), stop=(c == 15)), [yT_k[tau % 2], G[c // 2]], [fo])

                def E2a(tau):
                    s_ = rot("st", st)
                    for cg in range(4):
                        fo = FB[cg]
                        jk = tbf[cg % 2]
                        P.op("act", lambda e, fo=fo, jk=jk, cg=cg, s_=s_: e.activation(
                            out=jk[:, :], in_=fo[:, :], func=AF.Square, accum_out=s_[:, 4 + cg:5 + cg]), [fo], [jk, s_])
                    P.op("dve", lambda e, s_=s_: e.reduce_sum(out=s_[:, 3:4], in_=s_[:, 4:8], axis=AX.X), [s_], [s_])
                    rstd_from_ss(s_[:, 3:4], s_, D, 1, s_)
                    for cg in range(4):
                        fo = FB[cg]
                        ta_ = Ttmp[cg]
                        P.op("dve", lambda e, fo=fo, ta_=ta_, cg=cg, s_=s_: e.scalar_tensor_tensor(
                            out=ta_[:, :], in0=fo[:, :], scalar=s_[:, 0:1], in1=GN[:, cg * 512:(cg + 1) * 512],
                            op0=ALU.mult, op1=ALU.mult), [fo, s_, GN], [ta_])

                def XLOAD(tau):
                    if l == 0:
                        dma_in(xt[:, :], x_src(tau), [xt])
                    else:
                        dma_in(xt[:, :], x1_d[(tau - 2) * 128:(tau - 1) * 128, :], [xt], r=[x1_k])

                mst = {}

                def E2b(tau):
                    for cg in range(4):
                        ta_ = Ttmp[cg]
                        P.op("pool", lambda e, ta_=ta_, cg=cg: e.tensor_tensor(
                            out=xt[:, cg * 512:(cg + 1) * 512], in0=xt[:, cg * 512:(cg + 1) * 512], in1=ta_[:, :],
                            op=ALU.add), [ta_, xt], [xt])
                    if last:
                        P.dma("sp", out_d[(tau - 2) * 128:(tau - 1) * 128, :], xt[:, :], reads=[xt])
                    else:
                        if tau >= 2:
                            P.dma("sp", x1_d[(tau - 2) * 128:(tau - 1) * 128, :], xt[:, :], reads=[xt], writes=[x1_k])
                        s_ = rot("st", st)
                        P.op("act", lambda e, s_=s_: e.activation(out=xn[:, 0:D], in_=xt[:, :], func=AF.Square,
                                                                  accum_out=s_[:, 4:5]), [xt], [xn, s_])
                        rstd_from_ss(s_[:, 4:5], s_, D, 1, s_)
                        P.op("dve", lambda e, s_=s_: e.tensor_scalar(out=xn[:, 0:D], in0=xt[:, :], scalar1=s_[:, 0:1],
                                                                    scalar2=None, op0=ALU.mult), [xt, s_], [xn])

                def M2(tau):
                    var = 1 if tau < 2 else 0
                    for half in range(2):
                        hb = HB[half]
                        for c8 in range(8):
                            c = half * 8 + c8
                            P.op("pe", lambda e, hb=hb, c=c, c8=c8: e.transpose(
                                hb[:, c8 * 128:(c8 + 1) * 128], xn[:, c * 128:(c + 1) * 128], ident_b), [xn, cb], [hb])
                        for c8 in range(8):
                            c = half * 8 + c8
                            if half == 0:
                                P.op("act", lambda e, hb=hb, c=c, c8=c8, var=var, tau=tau, l=l: e.activation(
                                    out=hT[:, c, tau * 128:(tau + 1) * 128], in_=hb[:, c8 * 128:(c8 + 1) * 128],
                                    func=AF.Identity, scale=Amod[:, l + 1, var, c:c + 1], bias=Shm[:, l + 1, var, c:c + 1]),
                                    [hb, Amod, Shm], [hT_k[tau]])
                            else:
                                P.op("dve", lambda e, hb=hb, c=c, c8=c8, var=var, tau=tau, l=l: e.tensor_scalar(
                                    out=hT[:, c, tau * 128:(tau + 1) * 128], in0=hb[:, c8 * 128:(c8 + 1) * 128],
                                    scalar1=Amod[:, l + 1, var, c:c + 1], scalar2=Shm[:, l + 1, var, c:c + 1],
                                    op0=ALU.mult, op1=ALU.add), [hb, Amod, Shm], [hT_k[tau]])

                cur_var = None
                for i_, tau in enumerate(tiles):
                    var = 1 if tau < 2 else 0
                    if var != cur_var:
                        cur_var = var
                        gn_build(var)
                    if i_ == 0:
                        YLOAD(tau)
                    if i_ + 1 < len(tiles):
                        YLOAD(tiles[i_ + 1])
                    XLOAD(tau)
                    if i_ == 0:
                        E1a(tau)
                    E1b(tau)
                    E2a(tau)
                    if i_ >= 1 and not last:
                        M2(tiles[i_ - 1])
                    if i_ + 1 < len(tiles):
                        E1a(tiles[i_ + 1])
                    E2b(tau)
                if not last:
                    M2(tiles[-1])
                if l == 0:
                    ck(9, [(x1_d[0:128, 0:1024], 1024), (x1_d[1920:2048, 1024:2048], 1024), (hT[:, 0, 0:512], 512), (hT[:, 15, 1792:2304], 512)])
                P.barrier()

        try:
            _body()
        except _Stop:
            pass
        P.finish()
    return nc


def _consts():
    p = np.arange(128)[:, None]
    j = np.arange(128)[None, :]
    c = np.zeros((128, 1024), np.float32)
    c[:, 0:128] = (p == j)
    c[:, 128:256] = (p <= j)
    c[:, 256:384] = (p >= j)
    c[:, 384:512] = 1.0
    c[:, 512:640] = np.where(p <= j, 0.0, NEG)
    c[:, 640:768] = np.where(p >= j, 0.0, NEG)
    c[:, 768:896] = ((p // 64) == (j // 64))
    c[:, 896:1024] = (p == (j ^ 1))
    return c


def _rope():
    t = np.arange(T)
    row = (t // 64).astype(np.float32)
    col = (t % 64).astype(np.float32)
    inv = (np.float32(10000.0) ** (-np.arange(16, dtype=np.float32) / np.float32(16))).astype(np.float32)
    ang = np.concatenate([row[:, None] * inv[None, :], col[:, None] * inv[None, :]], axis=1).astype(np.float32)
    cos = np.cos(ang.astype(np.float64))
    sin = np.sin(ang.astype(np.float64))
    tab = np.zeros((2, 128, T), np.float32)
    for pp in range(128):
        d = pp % 64
        i = d // 2
        tab[0, pp] = cos[:, i]
        tab[1, pp] = -sin[:, i] if d % 2 == 0 else sin[:, i]
    return tab


def _slots(w, idx_list):
    ws = w[:, np.concatenate(idx_list)]
    ns = len(idx_list)
    return np.ascontiguousarray(ws.reshape(16, 128, ns, 256).transpose(2, 1, 0, 3).reshape(ns, 128, 4096))


_NC_CACHE = {}


def _prep(x, c, ctx, c_ctx, w_ada, b_ada, norm_pre, norm_post, w_in, w_out,
          lam_q1, lam_k1, lam_q2, lam_k2, attn_subln, conv_w, conv_b,
          i_bias, f_bias, mlstm_norm, cores=range(8)):
    f = lambda a: np.asarray(a, dtype=np.float32)
    x, c, ctx, c_ctx, w_ada, b_ada, norm_pre, norm_post, w_in, w_out = map(
        f, (x, c, ctx, c_ctx, w_ada, b_ada, norm_pre, norm_post, w_in, w_out))
    lam_q1, lam_k1, lam_q2, lam_k2, attn_subln, conv_w, conv_b, i_bias, f_bias, mlstm_norm = map(
        f, (lam_q1, lam_k1, lam_q2, lam_k2, attn_subln, conv_w, conv_b, i_bias, f_bias, mlstm_norm))
    A = np.arange
    idx = []
    for h in range(8):
        idx.append(np.concatenate([h * 128 + A(128), 1024 + h * 128 + A(128)]))
        idx.append(np.concatenate([2048 + h * 128 + A(128), 3072 + h * 128 + A(128)]))
    for hd in range(4):
        idx.append(np.concatenate([4096 + hd * 128 + A(128), 4608 + hd * 128 + A(128)]))
        idx.append(5120 + hd * 256 + A(256))
        idx.append(6144 + hd * 256 + A(256))
        idx.append(7168 + hd * 256 + A(256))
    win = np.stack([_slots(w_in[l], idx) for l in range(DEPTH)])
    wada = np.stack([_slots(w_ada[l], [s * 256 + A(256) for s in range(NSLOT_ADA)]) for l in range(DEPTH)])
    wgate = np.stack([np.ascontiguousarray(w_in[l][:, 8192:8208].reshape(16, 128, 16).transpose(1, 0, 2).reshape(128, 256))
                      for l in range(DEPTH)])
    wout = np.stack([np.ascontiguousarray(w_out[l].reshape(16, 128, D).transpose(1, 0, 2).reshape(128, 16 * D))
                     for l in range(DEPTH)])
    fm = lambda v, n: np.ascontiguousarray(v.reshape(n, 128).T)
    pfm = np.zeros((DEPTH, 128, 112), np.float32)
    prow = np.zeros((DEPTH, 400), np.float32)
    for l in range(DEPTH):
        pfm[l, :, 0:48] = fm(b_ada[l], 48)
        pfm[l, :, 48:64] = fm(norm_pre[l], 16)
        pfm[l, :, 64:80] = fm(norm_post[l], 16)
        pfm[l, :, 80:104] = conv_w[l].reshape(3, 8, 128).transpose(2, 1, 0).reshape(128, 24)
        pfm[l, :, 104:112] = fm(conv_b[l], 8)
        prow[l, 0:128] = attn_subln[l]
        prow[l, 128:136] = i_bias[l].reshape(8)
        prow[l, 136:144] = f_bias[l].reshape(8)
        prow[l, 144:208] = lam_q1[l]
        prow[l, 208:272] = lam_k1[l]
        prow[l, 272:336] = lam_q2[l]
        prow[l, 336:400] = lam_k2[l]
    mnorm = np.ascontiguousarray(mlstm_norm.reshape(DEPTH, 4, 256))
    cst = _consts()
    rope = _rope()
    in_maps = []
    for b in cores:
        c2 = np.stack([c[b], c_ctx], axis=0)
        c2fm = np.ascontiguousarray(c2.reshape(2, 16, 128).transpose(2, 1, 0).reshape(128, 32))
        in_maps.append({
            "x": np.ascontiguousarray(x[b]), "ctx": np.ascontiguousarray(ctx[b]), "c2fm": c2fm,
            "wada": wada, "win": win, "wgate": wgate, "wout": wout, "pfm": pfm, "prow": prow,
            "mnorm": mnorm, "cst": cst, "rope": rope,
        })
    return in_maps


def kernel(**inputs):
    in_maps = _prep(**inputs)
    if "nc" not in _NC_CACHE:
        _NC_CACHE["nc"] = build_program()
    nc = _NC_CACHE["nc"]
    res = run_bass_kernel_spmd(nc, in_maps, core_ids=list(range(8)))
    return np.stack([np.asarray(r["out"], dtype=np.float32) for r in res.results], axis=0)
```
